# Optimizing a Trainium2 kernel written in Bass

```python
import jax, jax.numpy as jnp
from jax import lax
import numpy as np

D_MODEL = 1024
BATCH = 8
SEQ = 4096
DEPTH = 4

GRID_W = 64
CTX_LEN = 256
MIX_WIDTH = D_MODEL
POOL_WIDTH = MIX_WIDTH // 2
N_POOL_GROUPS = 4
POOL_GROUP_DIM = POOL_WIDTH // N_POOL_GROUPS
POOL_WINDOWS = (2, 4, 8, 16)
ATTN_WIDTH = MIX_WIDTH - POOL_WIDTH
N_HEADS = 8
HEAD_DIM = ATTN_WIDTH // N_HEADS
NA_KH = 8
NA_KW = 16
NA_QB = 16
NA_KB = 32
NA_NCB = GRID_W // NA_QB
D_FF = 4 * D_MODEL
N_MOD = 6
EPS = 1e-6

kernel_name = 'hybrid_pool_natten_dit_block'


def _rmsnorm(x, g):
    x32 = x.astype(jnp.float32)
    y = x32 * lax.rsqrt(jnp.mean(x32 * x32, axis=-1, keepdims=True) + EPS)
    return (y * g.astype(jnp.float32)).astype(x.dtype)


def _modulate(x, gain, shift, scale):
    return _rmsnorm(x, gain) * (1 + scale) + shift


def _project(h, w_in_l):
    p = h @ w_in_l
    B, L, _ = p.shape
    u = p[..., :POOL_WIDTH]
    q, k, v = jnp.split(p[..., POOL_WIDTH:], 3, axis=-1)
    shp = (B, L, N_HEADS, HEAD_DIM)
    return u, q.reshape(shp), k.reshape(shp), v.reshape(shp)


def _pool_mixer(u, pool_w, pool_scale):
    B, L, _ = u.shape
    u32 = u.reshape(B, L, N_POOL_GROUPS, POOL_GROUP_DIM).astype(jnp.float32)
    cs = jnp.concatenate([jnp.zeros_like(u32[:, :1]), jnp.cumsum(u32, axis=1)], axis=1)
    t = jnp.arange(L)
    diffs = []
    for g, w in enumerate(POOL_WINDOWS):
        lo = w // 2
        hi = w - 1 - lo
        start = jnp.clip(t - lo, 0, L)
        end = jnp.clip(t + hi + 1, 0, L)
        count = (end - start).astype(jnp.float32)[None, :, None]
        diffs.append((cs[:, end, g] - cs[:, start, g]) / count - u32[:, :, g])
    d = jnp.stack(diffs, axis=2).astype(u.dtype)
    y = jnp.einsum('blgc,gcd->blgd', d, pool_w)
    return y.reshape(B, L, POOL_WIDTH) * pool_scale


def _na_column_tables():
    qcol = np.arange(GRID_W).reshape(NA_NCB, NA_QB)
    q_start = np.clip(qcol - NA_KW // 2, 0, GRID_W - NA_KW)
    kc0 = np.clip(np.arange(NA_NCB) * NA_QB - NA_KW // 2, 0, GRID_W - NA_KB)
    col_idx = kc0[:, None] + np.arange(NA_KB)
    kc = col_idx[:, None, :]
    mask = (kc >= q_start[:, :, None]) & (kc < q_start[:, :, None] + NA_KW)
    dc = np.clip(kc - qcol[:, :, None] + NA_KW - 1, 0, 2 * NA_KW - 2)
    return col_idx.astype(np.int32), mask, dc.astype(np.int32)


def _neighbourhood_attention(q, k, v, k_ctx, v_ctx, rpb):
    B, L, H, Dh = q.shape
    rows = L // GRID_W
    kh = min(NA_KH, rows)
    col_np, mask_np, dc_np = _na_column_tables()
    col_idx = jnp.asarray(col_np)
    col_mask = jnp.asarray(mask_np)[:, :, None, :]
    dc_idx = jnp.asarray(dc_np)[:, :, None, :]
    qg = q.reshape(B, rows, GRID_W, H, Dh)
    kg = k.reshape(B, rows, GRID_W, H, Dh)
    vg = v.reshape(B, rows, GRID_W, H, Dh)
    scale = Dh ** -0.5
    n_loc = kh * NA_KB

    def row_block(r):
        rs = jnp.clip(r - kh // 2, 0, rows - kh)
        q_r = lax.dynamic_index_in_dim(qg, r, axis=1, keepdims=False).reshape(B, NA_NCB, NA_QB, H, Dh)
        k_b = lax.dynamic_slice_in_dim(kg, rs, kh, axis=1)[:, :, col_idx]
        v_b = lax.dynamic_slice_in_dim(vg, rs, kh, axis=1)[:, :, col_idx]
        s_loc = jnp.einsum('bnqhd,brnkhd->bhnqrk', q_r, k_b).astype(jnp.float32) * scale
        dr_idx = rs + jnp.arange(kh) - r + (NA_KH - 1)
        bias = rpb[:, dr_idx[None, None, :, None], dc_idx]
        s_loc = jnp.where(col_mask, s_loc + bias.astype(jnp.float32), -jnp.inf)
        s_ctx = jnp.einsum('bnqhd,bkhd->bhnqk', q_r, k_ctx).astype(jnp.float32) * scale
        s = jnp.concatenate([s_loc.reshape(B, H, NA_NCB, NA_QB, n_loc), s_ctx], axis=-1)
        p = jax.nn.softmax(s, axis=-1).astype(v.dtype)
        p_loc = p[..., :n_loc].reshape(B, H, NA_NCB, NA_QB, kh, NA_KB)
        p_ctx = p[..., n_loc:]
        o = (jnp.einsum('bhnqrk,brnkhd->bnqhd', p_loc, v_b)
             + jnp.einsum('bhnqk,bkhd->bnqhd', p_ctx, v_ctx))
        return o.reshape(B, GRID_W, H, Dh)

    out = lax.map(row_block, jnp.arange(rows))
    return jnp.moveaxis(out, 0, 1).reshape(B, L, H * Dh)


def _context_attention(q, k, v):
    B, C, H, Dh = q.shape
    s = jnp.einsum('bqhd,bkhd->bhqk', q, k).astype(jnp.float32) * (Dh ** -0.5)
    p = jax.nn.softmax(s, axis=-1).astype(v.dtype)
    return jnp.einsum('bhqk,bkhd->bqhd', p, v).reshape(B, C, H * Dh)


def _sq_relu_mlp(h, w1, w2):
    return jnp.square(jax.nn.relu(h @ w1)) @ w2


def setup_inputs(seed: int = 0) -> dict:
    key = jax.random.key(seed)
    ks = jax.random.split(key, 17)
    D = D_MODEL
    nrm = jax.random.normal
    return {
        'x': nrm(ks[0], (BATCH, SEQ, D), jnp.float32),
        'c': nrm(ks[1], (BATCH, D), jnp.float32),
        'ctx': nrm(ks[2], (BATCH, CTX_LEN, D), jnp.float32),
        'c_ctx': nrm(ks[3], (D,), jnp.float32),
        'w_mod': nrm(ks[4], (DEPTH, D, N_MOD * D), jnp.float32) * (0.5 * D ** -0.5),
        'b_mod': nrm(ks[5], (DEPTH, N_MOD * D), jnp.float32) * 0.02,
        'g_pre_mix': 1.0 + 0.1 * nrm(ks[6], (DEPTH, D), jnp.float32),
        'g_post_mix': 1.0 + 0.1 * nrm(ks[7], (DEPTH, D), jnp.float32),
        'g_pre_mlp': 1.0 + 0.1 * nrm(ks[8], (DEPTH, D), jnp.float32),
        'g_post_mlp': 1.0 + 0.1 * nrm(ks[9], (DEPTH, D), jnp.float32),
        'w_in': nrm(ks[10], (DEPTH, D, POOL_WIDTH + 3 * ATTN_WIDTH), jnp.float32) * D ** -0.5,
        'pool_w': nrm(ks[11], (DEPTH, N_POOL_GROUPS, POOL_GROUP_DIM, POOL_GROUP_DIM), jnp.float32) * POOL_GROUP_DIM ** -0.5,
        'pool_scale': 1.0 + 0.1 * nrm(ks[12], (DEPTH, POOL_WIDTH), jnp.float32),
        'rpb': 0.1 * nrm(ks[13], (DEPTH, N_HEADS, 2 * NA_KH - 1, 2 * NA_KW - 1), jnp.float32),
        'w_out': nrm(ks[14], (DEPTH, MIX_WIDTH, D), jnp.float32) * MIX_WIDTH ** -0.5,
        'w_mlp_in': nrm(ks[15], (DEPTH, D, D_FF), jnp.float32) * D ** -0.5,
        'w_mlp_out': nrm(ks[16], (DEPTH, D_FF, D), jnp.float32) * D_FF ** -0.5,
    }


def reference(x, c, ctx, c_ctx, w_mod, b_mod, g_pre_mix, g_post_mix, g_pre_mlp, g_post_mlp,
              w_in, pool_w, pool_scale, rpb, w_out, w_mlp_in, w_mlp_out):
    for l in range(DEPTH):
        last = l == DEPTH - 1
        mod_x = (jax.nn.silu(c) @ w_mod[l] + b_mod[l])[:, None, :]
        mod_c = jax.nn.silu(c_ctx) @ w_mod[l] + b_mod[l]
        sh1, sc1, gt1, sh2, sc2, gt2 = jnp.split(mod_x, N_MOD, axis=-1)
        csh1, csc1, cgt1, csh2, csc2, cgt2 = jnp.split(mod_c, N_MOD, axis=-1)

        h_x = _modulate(x, g_pre_mix[l], sh1, sc1)
        h_c = _modulate(ctx, g_pre_mix[l], csh1, csc1)
        u_x, q_x, k_x, v_x = _project(h_x, w_in[l])
        u_c, q_c, k_c, v_c = _project(h_c, w_in[l])
        attn_x = _neighbourhood_attention(q_x, k_x, v_x, k_c, v_c, rpb[l])
        mix_x = jnp.concatenate([_pool_mixer(u_x, pool_w[l], pool_scale[l]), attn_x], axis=-1) @ w_out[l]
        x = x + gt1 * _rmsnorm(mix_x, g_post_mix[l])
        y_x = _sq_relu_mlp(_modulate(x, g_pre_mlp[l], sh2, sc2), w_mlp_in[l], w_mlp_out[l])
        x = x + gt2 * _rmsnorm(y_x, g_post_mlp[l])

        if not last:
            attn_c = _context_attention(q_c, k_c, v_c)
            mix_c = jnp.concatenate([_pool_mixer(u_c, pool_w[l], pool_scale[l]), attn_c], axis=-1) @ w_out[l]
            ctx = ctx + cgt1 * _rmsnorm(mix_c, g_post_mix[l])
            y_c = _sq_relu_mlp(_modulate(ctx, g_pre_mlp[l], csh2, csc2), w_mlp_in[l], w_mlp_out[l])
            ctx = ctx + cgt2 * _rmsnorm(y_c, g_post_mlp[l])
    return x
```

```python
import numpy as np
import concourse.bass as bass
import concourse.mybir as mybir
from concourse.bass_utils import run_bass_kernel_spmd

F32 = mybir.dt.float32
BF16 = mybir.dt.bfloat16
AF = mybir.ActivationFunctionType
ALU = mybir.AluOpType

D = 1024
L = 4096
CTX = 256
DEPTH = 4
DFF = 4096
NH = 8
T = 512
NG = L // T
EPS = 1e-6
NEG = -30000.0
NDS = 26
GRID_W = 64


class Buf:
    __slots__ = ("name", "w", "r")

    def __init__(self, name):
        self.name = name
        self.w = {}
        self.r = {}


class Tl:
    def __init__(self, nc, name, shape, dtype, psum=False, off=None):
        if psum:
            self.t = nc.alloc_psum_tensor(name, shape, dtype)
        else:
            self.t = nc.alloc_sbuf_tensor_at(name, shape, dtype, offset=off)
        self.b = Buf(name)


class Sched:
    def __init__(self, nc):
        self.nc = nc
        self.E = {"pe": nc.tensor, "act": nc.scalar, "dve": nc.vector, "pool": nc.gpsimd, "sp": nc.sync}
        self.sem = {e: nc.alloc_semaphore("sem_" + e) for e in ("pe", "act", "dve", "pool")}
        self.cnt = {e: 0 for e in self.sem}
        self.ops = {e: [] for e in self.E}
        self.waited = {e: {} for e in self.E}
        self.dsem = [nc.alloc_semaphore("dsem%d" % i) for i in range(NDS)]
        self.dcnt = [0] * NDS
        self.dpool = {"sp": list(range(0, 16)), "io": [], "cast": list(range(16, 22)), "actq": list(range(22, 26))}
        self.dnext = {"sp": 0, "io": 0, "cast": 0, "actq": 0}

    def _semh(self, key):
        if isinstance(key, tuple):
            return self.dsem[key[1]]
        return self.sem[key]

    def _deps(self, reads, writes):
        evs = []
        for b in reads:
            evs += list(b.w.items())
        for b in writes:
            if isinstance(b, tuple):
                evs += list(b[0].r.items())
            else:
                evs += list(b.w.items()) + list(b.r.items())
        return evs

    def _waits(self, eng, evs):
        out = []
        w = self.waited[eng]
        for key, val in evs:
            if key == eng and eng == "pe":
                continue
            if w.get(key, 0) >= val:
                continue
            w[key] = val
            out.append((key, val))
        return out

    def _update(self, ev, reads, writes):
        k, v = ev
        for b in reads:
            b.r[k] = v
        for b in writes:
            if isinstance(b, tuple):
                b[0].w[k] = v
            else:
                b.w = {k: v}
                b.r = {}

    def op(self, eng, fn, reads=(), writes=()):
        waits = self._waits(eng, self._deps(reads, writes))
        self.cnt[eng] += 1
        ev = (eng, self.cnt[eng])
        sem = self.sem[eng]

        def thunk(e):
            for k, v in waits:
                e.wait_ge(self._semh(k), v)
            fn(e).then_inc(sem, 1)

        self.ops[eng].append(thunk)
        self._update(ev, reads, writes)

    def dma(self, q, out, in_, reads=(), writes=(), cls=None):
        cls = cls or {"sp": "sp", "pool": "cast", "act": "actq"}[q]
        pl = self.dpool[cls]
        i = pl[self.dnext[cls] % len(pl)]
        self.dnext[cls] += 1
        evs = self._deps(reads, writes)
        if self.dcnt[i]:
            evs.append((("d", i), self.dcnt[i]))
        waits = self._waits(q, evs)
        self.dcnt[i] += 16
        ev = (("d", i), self.dcnt[i])
        ds = self.dsem[i]

        def thunk(e):
            for k, v in waits:
                e.wait_ge(self._semh(k), v)
            e.dma_start(out=out, in_=in_).then_inc(ds, 16)

        self.ops[q].append(thunk)
        self._update(ev, reads, writes)

    def barrier(self):
        evs = [(e, c) for e, c in self.cnt.items() if c] + [(("d", i), self.dcnt[i]) for i in self.dpool["sp"] + self.dpool["actq"] if self.dcnt[i]]
        for eng in self.E:
            waits = self._waits(eng, [ev for ev in evs if ev[0] != eng])

            def thunk(e, waits=waits):
                for k, v in waits:
                    e.wait_ge(self._semh(k), v)

            self.ops[eng].append(thunk)


class Ring:
    def __init__(self, bufs, items, loader, alias=()):
        self.bufs = bufs
        self.items = items
        self.loader = loader
        self.alias = list(alias)
        self.nl = 0
        self.nu = 0

    def _load(self, extra=()):
        if self.nl < len(self.items):
            self.loader(self.items[self.nl], self.bufs[self.nl % len(self.bufs)], list(extra))
            self.nl += 1

    def prime(self):
        for _ in self.bufs:
            self._load([(b,) for b in self.alias])

    def prime_rest(self):
        while self.nl < len(self.bufs):
            self._load()

    def get(self):
        b = self.bufs[self.nu % len(self.bufs)]
        self.nu += 1
        return b

    def done(self):
        self._load()


def build(depth=DEPTH):
    nc = bass.Bass("TRN2", target_bir_lowering=False)
    S = Sched(nc)

    def dram(name, shape, dtype, kind):
        return nc.dram_tensor(name, shape, dtype, kind=kind).ap()

    xT_in = dram("xT", [D, L], F32, "ExternalInput")
    ctxT_in = dram("ctxT", [D, CTX], F32, "ExternalInput")
    cc_in = dram("cc", [128, 16], F32, "ExternalInput")
    ident_in = dram("ident", [128, 128], F32, "ExternalInput")
    invc_in = dram("invc", [128, 64], F32, "ExternalInput")
    wmod_in = dram("w_mod", [depth, D, 6 * D], F32, "ExternalInput")
    bmod_in = dram("b_modT", [depth, 128, 96], F32, "ExternalInput")
    gvec_in = dram("gvec", [depth, 128, 32], F32, "ExternalInput")
    pscale_in = dram("pscale", [depth, 128, 4], F32, "ExternalInput")
    win_in = dram("w_in", [depth, D, 2048], F32, "ExternalInput")
    poolw_in = dram("pool_w", [depth, 512, 128], F32, "ExternalInput")
    bias_in = dram("biasT", [depth, 128, 21 * 8 * 128], F32, "ExternalInput")
    wout_in = dram("w_out", [depth, D, D], F32, "ExternalInput")
    w1_in = dram("w1", [depth, D, DFF], F32, "ExternalInput")
    w2_in = dram("w2", [depth, DFF, D], F32, "ExternalInput")
    yT = dram("yT", [D, L], F32, "ExternalOutput")

    xs = dram("xs", [D, L], F32, "Internal")
    B_xs = [Buf("xs%d" % g) for g in range(NG)]
    uT = dram("uT", [4, 128, L + 16], F32, "Internal")
    B_uT = Buf("uT")
    uTb = dram("uTb", [4, 128, L], BF16, "Internal")
    B_uTb = Buf("uTb")
    WB = []
    for l in range(depth):
        WB.append(dict(
            wmod=dram("wmod_b%d" % l, [D, 6 * D], BF16, "Internal"), b_wmod=Buf("wmod"),
            win=dram("win_b%d" % l, [D, 2048], BF16, "Internal"), b_win=Buf("win"),
            wout=dram("wout_b%d" % l, [8, 128, 8 * 128], BF16, "Internal"), b_wout=Buf("wout"),
            w1=dram("w1_b%d" % l, [D, DFF], BF16, "Internal"), b_w1=Buf("w1"),
            w2=dram("w2_b%d" % l, [8, 128, 32 * 128], BF16, "Internal"), b_w2=Buf("w2"),
            pool=dram("pool_b%d" % l, [512, 128], BF16, "Internal"), b_pool=Buf("pool"),
            bias=dram("bias_b%d" % l, [128, 21 * 8 * 128], BF16, "Internal"), b_bias=Buf("bias"),
        ))

    ptr = [16512]

    def _sz(shape, dtype):
        n = 1
        for d_ in shape[1:]:
            n *= d_
        return n * (2 if dtype == BF16 else 4)

    def sb(name, shape, dtype, off=None):
        if off is None:
            off = ptr[0]
            ptr[0] = (off + _sz(shape, dtype) + 63) // 64 * 64
        return Tl(nc, name, shape, dtype, off=off)

    def union(nbytes):
        off = ptr[0]
        ptr[0] = (off + nbytes + 63) // 64 * 64
        return off

    def sub(base, items):
        out = []
        o = base
        for name, shape, dtype in items:
            out.append(Tl(nc, name, shape, dtype, off=o))
            o = (o + _sz(shape, dtype) + 63) // 64 * 64
        return out, o - base

    ident = sb("ident", [128, 128], BF16)
    ones = sb("ones", [128, 128], BF16)
    epsc = sb("epsc", [128, 1], F32)
    invc = sb("invc_t", [128, 2, 4, 8], F32)
    cc = sb("cc_t", [128, 16], F32)
    sT = sb("sT", [128, 16], BF16)
    bmod = sb("bmod", [128, 96], F32)
    gvec = sb("gvec_t", [128, 32], F32)
    pscale = sb("pscale_t", [128, 4], F32)
    modT = sb("modT", [128, 96], F32)
    A1 = sb("A1", [128, 8, 2], F32)
    G1 = sb("G1", [128, 8, 2], F32)
    A2 = sb("A2", [128, 8, 2], F32)
    G2 = sb("G2", [128, 8, 2], F32)
    modT_b = sb("modT_b", [128, 96], F32)
    A1_b = sb("A1_b", [128, 8, 2], F32)
    G1_b = sb("G1_b", [128, 8, 2], F32)
    A2_b = sb("A2_b", [128, 8, 2], F32)
    G2_b = sb("G2_b", [128, 8, 2], F32)
    ctxT = sb("ctxT_t", [128, 8, CTX], F32)
    sq2 = [sb("sq2_%d" % i, [128, T], BF16) for i in range(3)]
    rstd = sb("rstd", [128, T], F32)
    tmpf_off = ptr[0]
    tmpf = [sb("tmpf%d" % i, [128, T], F32) for i in range(2)]
    umix = union(16384)
    mixg = Tl(nc, "mixg", [128, 8, T], F32, off=umix)
    mixg.cb = [Buf("mixg%d" % i) for i in range(8)]
    xgA2 = Tl(nc, "xgA2", [128, 8, T], F32, off=umix)
    qcT = sb("qcT", [128, 4, CTX], BF16)
    kcT = sb("kcT", [128, 4, CTX], BF16)
    Vcaug = sb("Vcaug", [128, CTX // 128, 8, 65], BF16)
    ucT_off = ptr[0]
    ucT = sb("ucT", [128, 4, CTX + 16], F32)
    rtmp = tmpf + [Tl(nc, "rtmp2", [128, T], F32, off=ucT_off), Tl(nc, "rtmp3", [128, T], F32, off=ucT_off + 2048)]
    rinv = sb("rinv", [128, 8], F32)
    On = sb("On", [128, 512], BF16)
    poolT = sb("poolT", [128, 4, 128], BF16)
    dT = sb("dT", [128, 4, T], BF16)
    dT.cb = [Buf("dT%d" % i) for i in range(4)]
    dU = Tl(nc, "dU", [128, 4, T], BF16, off=tmpf_off)
    poolTs = sb("poolTs", [128, 4, 128], BF16)
    edg = sb("edg", [128, 8], F32)
    u5 = union(33280)
    Vaug = Tl(nc, "Vaug", [128, L // 128, 8, 65], BF16, off=u5)
    (xgC2, hTC2), n_ = sub(u5, [("xgC2", [128, 8, T], F32), ("hTC2", [128, 8, T], BF16)])
    hTC2.cb = [Buf("hTC2_%d" % i) for i in range(8)]
    assert n_ <= 33280
    ux = union(16384)
    xg = Tl(nc, "xg", [128, 8, T], F32, off=ux)
    (xr0, xr1, xr2, ug0), n_ = sub(ux, [("xr0", [128, T], F32), ("xr1", [128, T], F32), ("xr2", [128, T], F32),
                                        ("ug0", [128, 4, T + 16], F32)])
    assert n_ <= 16384
    xr = [xr0, xr1, xr2]
    u1 = union(8192)
    hT = Tl(nc, "hT", [128, 8, T], BF16, off=u1)
    hT_default = hT
    hT.cb = [Buf("hT_%d" % i) for i in range(8)]
    catT = Tl(nc, "catT", [128, 8, T], BF16, off=u1)
    u2 = union(24576)
    wpan, n_ = sub(u2, [("wpan%d" % i, [128, 8, 512], BF16) for i in range(3)])
    (bias_int, bias_bnd, pT0, pT1, pT2), n_ = sub(u2, [("bias_int", [128, 5, 8, 128], BF16), ("bias_bnd", [128, 4, 8, 128], BF16),
                                                       ("pT0", [128, 7 * 128], BF16), ("pT1", [128, 7 * 128], BF16),
                                                       ("pT2", [128, 7 * 128], BF16)])
    assert n_ <= 24576, n_
    pT = [pT0, pT1, pT2]
    u3 = union(32768)
    qT = Tl(nc, "qT", [128, 4, L], BF16, off=u3)
    actT = Tl(nc, "actT", [128, 32, T], BF16, off=u3)
    u4 = union(32768)
    kT = Tl(nc, "kT", [128, 4, L], BF16, off=u4)
    w2pan, n_ = sub(u4, [("w2pan%d" % i, [128, 32, 128], BF16) for i in range(3)])
    wpan4 = Tl(nc, "wpan4", [128, 8, 512], BF16, off=u4 + 24576)
    u6 = union(10432)
    uev = Tl(nc, "uev", [128, 4, T], F32, off=u6)
    (pta, ptb, wop0, wop1, wop2), n_ = sub(u6, [("pta", [128, T + 16], F32), ("ptb", [128, T + 16], F32),
                                                ("wop0", [128, 8, 128], BF16), ("wop1", [128, 8, 128], BF16),
                                                ("wop2", [128, 8, 128], BF16)])
    assert n_ <= 10432, n_
    wop = [wop0, wop1, wop2]
    sqC = Tl(nc, "sqC", [128, 8, T], BF16, off=u6)
    sqC.cb = [Buf("sqC%d" % i) for i in range(8)]
    wpan5 = Tl(nc, "wpan5", [128, 8, 512], BF16, off=u5 + 24576)
    assert ptr[0] <= nc.SBUF_PARTITION_SIZE_BYTES, ptr[0]

    psS = [Tl(nc, "psS%d" % i, [128, 1024], F32, psum=True) for i in range(2)]
    psO = [Tl(nc, "psO%d" % i, [128, 512], F32, psum=True) for i in range(2)]
    psT = Tl(nc, "psT", [128, 1024], BF16, psum=True)
    psX = Tl(nc, "psX", [128, 512], F32, psum=True)
    BK = [Buf("bank%d" % i) for i in range(6)]
    ROT = [(psS[0].t[:, 0:512], BK[0]), (psS[0].t[:, 512:1024], BK[1]),
           (psS[1].t[:, 0:512], BK[2]), (psS[1].t[:, 512:1024], BK[3]),
           (psO[0].t[:, :], BK[4]), (psO[1].t[:, :], BK[5])]
    rot_i = [0]

    def rot():
        r = ROT[rot_i[0] % 6]
        rot_i[0] += 1
        return r

    S.dma("pool", out=ident.t[:], in_=ident_in, writes=[ident.b])
    S.dma("sp", out=invc.t[:].rearrange("p a b c -> p (a b c)"), in_=invc_in, writes=[invc.b])
    S.dma("sp", out=cc.t[:], in_=cc_in, writes=[cc.b])
    S.dma("sp", out=ctxT.t[:], in_=ctxT_in.rearrange("(k p) t -> p k t", p=128), writes=[ctxT.b])
    S.op("dve", lambda e: e.memset(ones.t[:], 1.0), writes=[ones.b])
    S.op("dve", lambda e: e.memset(epsc.t[:], EPS), writes=[epsc.b])
    S.op("dve", lambda e: e.memset(rstd.t[:, 0:32], 0.0), writes=[rstd.b])
    S.op("dve", lambda e: e.memset(Vaug.t[:], 1.0), writes=[Vaug.b])
    S.op("dve", lambda e: e.memset(Vcaug.t[:], 1.0), writes=[Vcaug.b])
    S.op("dve", lambda e: e.memset(ucT.t[:], 0.0), writes=[ucT.b])
    S.op("act", lambda e: e.activation(out=sT.t[:], in_=cc.t[:], func=AF.Silu), reads=[cc.b], writes=[sT.b])
    zsrc = rstd.t[:, 0:32].rearrange("p (g c) -> p g c", c=8)
    S.dma("sp", out=uT[:, :, 0:8].rearrange("g p c -> p g c"), in_=zsrc, reads=[rstd.b], writes=[(B_uT,)])
    S.dma("sp", out=uT[:, :, L + 8:L + 16].rearrange("g p c -> p g c"), in_=zsrc, reads=[rstd.b], writes=[(B_uT,)])

    pending = []

    def queue_casts(l):
        W = WB[l]

        def add(name, out, in_):
            pending.append((l, name, lambda: S.dma("pool", out=out, in_=in_, writes=[(W["b_" + name],)])))
        for i in range(4):
            add("win", W["win"][i * 256:(i + 1) * 256, :], win_in[l, i * 256:(i + 1) * 256, :])
        for i in range(16):
            add("wmod", W["wmod"][i * 64:(i + 1) * 64, :], wmod_in[l, i * 64:(i + 1) * 64, :])
        add("pool", W["pool"], poolw_in[l])
        for i in range(4):
            add("bias", W["bias"][i * 32:(i + 1) * 32, :], bias_in[l, i * 32:(i + 1) * 32, :])
        for m in range(8):
            add("wout", W["wout"][m].rearrange("p (k c) -> p k c", c=128),
                wout_in[l, :, m * 128:(m + 1) * 128].rearrange("(k p) c -> p k c", p=128))
        for i in range(8):
            add("w1", W["w1"][i * 128:(i + 1) * 128, :], w1_in[l, i * 128:(i + 1) * 128, :])
        for m in range(8):
            for jq in range(2):
                add("w2", W["w2"][m].rearrange("p (j c) -> p j c", c=128)[:, jq * 16:(jq + 1) * 16, :],
                    w2_in[l, jq * 2048:(jq + 1) * 2048, m * 128:(m + 1) * 128].rearrange("(j p) c -> p j c", p=128))

    def tick(k):
        for _ in range(k):
            if pending:
                pending.pop(0)[2]()

    def need(l, name):
        while any(p[0] == l and p[1] == name for p in pending):
            pending.pop(0)[2]()

    for l_ in range(depth):
        queue_casts(l_)

    sq_i = [0]

    def stats_sq(m, src_ap, srcb, Tn, eng="pool"):
        sq = sq2[sq_i[0] % 3]
        sq_i[0] += 1
        if eng == "dve":
            S.op("dve", lambda e: e.tensor_tensor(out=sq.t[:, :Tn], in0=src_ap, in1=src_ap, op=ALU.mult),
                 reads=[srcb], writes=[sq.b])
        else:
            S.op("pool", lambda e: e.tensor_tensor(out=sq.t[:, :Tn], in0=src_ap, in1=src_ap, op=ALU.mult),
                 reads=[srcb], writes=[sq.b])

        def acc():
            S.op("pe", lambda e: e.matmul(psX.t[:, :Tn], lhsT=ones.t[:], rhs=sq.t[:, :Tn], start=(m == 0), stop=(m == 7)),
                 reads=[sq.b, ones.b], writes=[psX.b] if m == 0 else [(psX.b,)])
        return acc

    def stats_sq_acc(m, src_ap, srcb, Tn):
        stats_sq(m, src_ap, srcb, Tn)()

    def stats_finish(Tn):
        S.op("act", lambda e: e.activation(out=rstd.t[:, :Tn], in_=psX.t[:, :Tn], func=AF.Ln, bias=epsc.t[:, 0:1],
                                           scale=1.0 / D), reads=[psX.b, epsc.b], writes=[rstd.b])
        S.op("act", lambda e: e.activation(out=rstd.t[:, :Tn], in_=rstd.t[:, :Tn], func=AF.Exp, scale=-0.5),
             reads=[rstd.b], writes=[rstd.b])

    def stats(src, Tn, srcb, alt=False):
        for k in range(8):
            stats_sq(k, src[:, k, :Tn], srcb, Tn, eng=("dve" if (alt and k % 2) else "pool"))()
        stats_finish(Tn)

    def modulate_chunk(k, src, srcb, Tn, A, Bm, j, hT):
        tf = tmpf[k % 2]
        S.op("dve", lambda e: e.tensor_tensor(out=tf.t[:, :Tn], in0=src[:, k, :Tn], in1=rstd.t[:, :Tn], op=ALU.mult),
             reads=[srcb, rstd.b], writes=[tf.b])
        S.op("act", lambda e: e.activation(out=hT.t[:, k, :Tn], in_=tf.t[:, :Tn], func=AF.Identity, bias=Bm(k),
                                           scale=A.t[:, k, j:j + 1]),
             reads=[tf.b, A.b, A.modb], writes=[hT.cb[k]])

    def modulate(src, srcb, Tn, A, Bm, j, hT=None):
        hT = hT or hT_default
        for k in range(8):
            modulate_chunk(k, src, srcb, Tn, A, Bm, j, hT)

    MODS = [(modT, A1, G1, A2, G2, modT.t[:].rearrange("p (w k j) -> p w k j", w=6, k=8, j=2)),
            (modT_b, A1_b, G1_b, A2_b, G2_b, modT_b.t[:].rearrange("p (w k j) -> p w k j", w=6, k=8, j=2))]

    def residual_update(src_y, G, j, xres, xresb, Tn, dst, dstb, tmps=None):
        for m in range(8):
            S.op("dve", lambda e, m=m: e.scalar_tensor_tensor(
                out=src_y.t[:, m, :Tn], in0=src_y.t[:, m, :Tn], scalar=G.t[:, m, j:j + 1], in1=rstd.t[:, :Tn],
                op0=ALU.mult, op1=ALU.mult), reads=[src_y.cb[m], G.b, rstd.b], writes=[src_y.cb[m]])
            S.op("pool", lambda e, m=m: e.tensor_tensor(out=dst[:, m, :Tn], in0=src_y.t[:, m, :Tn], in1=xres[:, m, :Tn],
                                                        op=ALU.add), reads=[src_y.cb[m], xresb], writes=[(dstb,)])

    def phaseM(l):
        need(l, "wmod")
        W = WB[l]
        modT, A1, G1, A2, G2, mod4 = MODS[l % 2]
        for a_ in (A1, G1, A2, G2):
            a_.modb = modT.b
        S.dma("sp", out=bmod.t[:], in_=bmod_in[l], writes=[bmod.b])
        S.dma("sp", out=gvec.t[:], in_=gvec_in[l], writes=[gvec.b])
        S.dma("sp", out=pscale.t[:], in_=pscale_in[l], writes=[pscale.b])
        ringM = Ring(wpan, list(range(12)), lambda piece, wm, ex: S.dma(
            "sp", out=wm.t[:], in_=W["wmod"][:, piece * 512:(piece + 1) * 512].rearrange("(k p) n -> p k n", p=128),
            reads=[W["b_wmod"]], writes=[wm.b]))
        ringM.prime()
        for piece in range(12):
            wm = ringM.get()

            def fn(e, piece=piece, wm=wm):
                for nn in range(4):
                    nch = piece * 4 + nn
                    for k in range(8):
                        r = e.matmul(psX.t[:, nch * 2:nch * 2 + 2], lhsT=wm.t[:, k, nn * 128:(nn + 1) * 128],
                                     rhs=sT.t[:, 2 * k:2 * k + 2], start=(k == 0), stop=(k == 7))
                return r
            S.op("pe", fn, reads=[wm.b, sT.b], writes=[psX.b] if piece == 0 else [(psX.b,)])
            ringM.done()
        S.op("dve", lambda e: e.tensor_tensor(out=modT.t[:], in0=psX.t[:, 0:96], in1=bmod.t[:], op=ALU.add),
             reads=[psX.b, bmod.b], writes=[modT.b])
        for j in range(2):
            S.op("dve", lambda e, j=j: e.scalar_tensor_tensor(out=A1.t[:, :, j], in0=mod4[:, 1, :, j], scalar=1.0,
                                                              in1=gvec.t[:, 0:8], op0=ALU.add, op1=ALU.mult),
                 reads=[modT.b, gvec.b], writes=[(A1.b,)])
            S.op("dve", lambda e, j=j: e.tensor_tensor(out=G1.t[:, :, j], in0=mod4[:, 2, :, j], in1=gvec.t[:, 8:16],
                                                       op=ALU.mult), reads=[modT.b, gvec.b], writes=[(G1.b,)])
            S.op("dve", lambda e, j=j: e.scalar_tensor_tensor(out=A2.t[:, :, j], in0=mod4[:, 4, :, j], scalar=1.0,
                                                              in1=gvec.t[:, 16:24], op0=ALU.add, op1=ALU.mult),
                 reads=[modT.b, gvec.b], writes=[(A2.b,)])
            S.op("dve", lambda e, j=j: e.tensor_tensor(out=G2.t[:, :, j], in0=mod4[:, 5, :, j], in1=gvec.t[:, 24:32],
                                                       op=ALU.mult), reads=[modT.b, gvec.b], writes=[(G2.b,)])

    for l in range(depth):
        W = WB[l]
        last = (l == depth - 1)
        need(l, "win")
        modT, A1, G1, A2, G2, mod4 = MODS[l % 2]
        if l == 0:
            phaseM(0)
            S.barrier()

        x_src = xT_in if l == 0 else xs
        if l > 0:
            S.op("dve", lambda e: e.memset(Vaug.t[:, :, :, 64:65], 1.0), writes=[Vaug.b])
            S.op("dve", lambda e: e.memset(ucT.t[:], 0.0), writes=[ucT.b])
        ringA = Ring(wpan, [pn for g in range(NG + 1) for pn in range(4)], lambda pn, wp, ex: S.dma(
            "sp", out=wp.t[:], in_=W["win"][:, pn * 512:(pn + 1) * 512].rearrange("(k p) n -> p k n", p=128),
            reads=[W["b_win"]], writes=[wp.b]))
        ringA.prime()
        xgA = [xg, xgA2]

        def loadA(g):
            if g == NG:
                return ctxT
            src = xgA[g % 2]
            S.dma("sp", out=src.t[:], in_=x_src[:, g * T:(g + 1) * T].rearrange("(k p) t -> p k t", p=128),
                  reads=[B_xs[g]], writes=[src.b])
            return src

        def prologueA(g):
            src = loadA(g)
            stats(src.t, CTX if g == NG else T, src.b, alt=True)
            return src

        srcT = prologueA(0)
        for g in range(NG + 1):
            isc = (g == NG)
            Tn = CTX if isc else T
            j = 1 if isc else 0
            src, srcb = srcT.t, srcT.b
            nxtT = loadA(g + 1) if g + 1 <= NG else None
            if g == 0:
                modulate(src, srcb, Tn, A1, lambda k, j=j, mod4=mod4: mod4[:, 0, k, j:j + 1], j)
            for pn in range(4):
                wp = ringA.get()
                if pn < 3:
                    for m in range(4):
                        ps, psb = rot()

                        def fn(e, m=m, ps=ps, wp=wp, Tn=Tn):
                            for k in range(8):
                                r = e.matmul(ps[:, :Tn], lhsT=wp.t[:, k, m * 128:(m + 1) * 128], rhs=hT.t[:, k, :Tn],
                                             start=(k == 0), stop=(k == 7))
                            return r
                        S.op("pe", fn, reads=[wp.b] + hT.cb, writes=[psb])
                        if pn == 0:
                            if isc:
                                S.op("dve", lambda e, m=m, ps=ps: e.tensor_copy(out=ucT.t[:, m, 8:8 + CTX], in_=ps[:, :CTX]),
                                     reads=[psb], writes=[(ucT.b,)])
                            else:
                                S.op("dve", lambda e, m=m, ps=ps: e.tensor_copy(out=uev.t[:, m, :], in_=ps[:, :]),
                                     reads=[psb], writes=[(uev.b,)])
                                S.op("dve", lambda e, m=m, ps=ps: e.tensor_copy(out=dT.t[:, m, :], in_=ps[:, :]),
                                     reads=[psb], writes=[dT.cb[m]])
                        elif pn == 1:
                            dst = qcT.t[:, m, :] if isc else qT.t[:, m, g * T:(g + 1) * T]
                            S.op("act", lambda e, ps=ps, dst=dst, Tn=Tn: e.activation(out=dst, in_=ps[:, :Tn], func=AF.Identity,
                                                                                      scale=0.125),
                                 reads=[psb], writes=[((qcT if isc else qT).b,)])
                        else:
                            dst = kcT.t[:, m, :] if isc else kT.t[:, m, g * T:(g + 1) * T]
                            S.op("dve", lambda e, ps=ps, dst=dst, Tn=Tn: e.tensor_copy(out=dst, in_=ps[:, :Tn]),
                                 reads=[psb], writes=[((kcT if isc else kT).b,)])
                    ringA.done()
                    if pn == 0 and not isc:
                        S.dma("sp", out=uT[:, :, 8 + g * T:8 + (g + 1) * T].rearrange("g p c -> p g c"), in_=uev.t[:],
                              reads=[uev.b], writes=[(B_uT,)])
                        S.dma("sp", out=uTb[:, :, g * T:(g + 1) * T].rearrange("g p c -> p g c"), in_=dT.t[:],
                              reads=dT.cb, writes=[(B_uTb,)])
                    if pn == 2 and g + 1 <= NG:
                        stats(nxtT.t, CTX if g + 1 == NG else T, nxtT.b, alt=True)
                        srcT = nxtT
                else:
                    ntt = Tn // 128
                    banks = [rot() for _ in range(ntt)]
                    jn = 1 if g + 1 == NG else 0
                    Tnn = CTX if g + 1 == NG else T
                    for k in range(8):
                        def fn(e, k=k, banks=banks, wp=wp, ntt=ntt):
                            for tt in range(ntt):
                                r = e.matmul(banks[tt][0][:, :], lhsT=hT.t[:, k, tt * 128:(tt + 1) * 128], rhs=wp.t[:, k, :],
                                             start=(k == 0), stop=(k == 7))
                            return r
                        S.op("pe", fn, reads=[wp.b, hT.cb[k]],
                             writes=[bk[1] for bk in banks] if k == 0 else [(bk[1],) for bk in banks])
                        if g + 1 <= NG:
                            modulate_chunk(k, srcT.t, srcT.b, Tnn, A1, lambda kk, jn=jn, mod4=mod4: mod4[:, 0, kk, jn:jn + 1], jn, hT)
                    for tt in range(ntt):
                        ps, psb = banks[tt]
                        Vt = Vcaug if isc else Vaug
                        vi = tt if isc else g * 4 + tt
                        S.op("act" if tt % 2 else "dve",
                             (lambda e, ps=ps, Vt=Vt, vi=vi: e.activation(
                                 out=Vt.t[:, vi, :, 0:64], in_=ps[:, :].rearrange("p (h d) -> p h d", d=64), func=AF.Identity))
                             if tt % 2 else
                             (lambda e, ps=ps, Vt=Vt, vi=vi: e.tensor_copy(
                                 out=Vt.t[:, vi, :, 0:64], in_=ps[:, :].rearrange("p (h d) -> p h d", d=64))),
                             reads=[psb], writes=[(Vt.b,)])
                    ringA.done()
            tick(2)
        S.barrier()

        for nm in ("pool", "bias", "wout"):
            need(l, nm)
        ngc = NG if last else NG + 1
        ring1 = Ring(wpan + [wpan4, wpan5], [jb for g in range(ngc) for jb in range(8)], lambda jb, wp, ex: S.dma(
            "sp", out=wp.t[:], in_=W["w1"][:, jb * 512:(jb + 1) * 512].rearrange("(k p) n -> p k n", p=128),
            reads=[W["b_w1"]], writes=[wp.b] + ex), alias=[bias_int.b, bias_bnd.b, pT[0].b, pT[1].b, pT[2].b, kT.b, Vaug.b])
        ring2 = Ring(w2pan, [m for g in range(ngc) for m in range(8)], lambda m, w2p, ex: S.dma(
            "act", out=w2p.t[:].rearrange("p j c -> p (j c)"), in_=W["w2"][m], reads=[W["b_w2"]], writes=[w2p.b] + ex),
            alias=[kT.b])
        S.dma("sp", out=poolT.t[:], in_=W["pool"].rearrange("(g p) n -> p g n", p=128), reads=[W["b_pool"]],
              writes=[poolT.b])
        for pg, w in enumerate((2, 4, 8, 16)):
            S.op("dve", lambda e, pg=pg, w=w: e.tensor_scalar(out=poolTs.t[:, pg, :], in0=poolT.t[:, pg, :], scalar1=1.0 / w,
                                                              scalar2=None, op0=ALU.mult),
                 reads=[poolT.b], writes=[poolTs.b] if pg == 0 else [(poolTs.b,)])
        S.op("dve", lambda e: e.tensor_scalar(out=poolT.t[:], in0=poolT.t[:], scalar1=-1.0, scalar2=None, op0=ALU.mult),
             reads=[poolT.b, poolTs.b], writes=[poolT.b])
        S.dma("sp", out=bias_int.t[:].rearrange("p a h q -> p (a h q)"), in_=W["bias"][:, 0:5 * 1024],
              reads=[W["b_bias"]], writes=[bias_int.b])
        S.op("act", lambda e: e.activation(out=bias_int.t[:].rearrange("p a h q -> p (a h q)"),
                                           in_=bias_int.t[:].rearrange("p a h q -> p (a h q)"), func=AF.Exp),
             reads=[bias_int.b], writes=[bias_int.b])

        def poolmix(usrc, usrcb, Tn, first, lastg, cast_u):
            Wd = Tn + 16
            for pg, w in enumerate((2, 4, 8, 16)):
                u = usrc[:, pg, :]
                levels = [(1, Wd, lambda i0, i1, u=u: (u[:, i0:i1], u[:, i0 - 1:i1 - 1]))]
                if w >= 4:
                    levels.append((2, Wd - 1, lambda i0, i1: (pta.t[:, i0 + 1:i1 + 1], pta.t[:, i0 - 1:i1 - 1])))
                if w >= 8:
                    levels.append((4, Wd - 3, lambda i0, i1: (ptb.t[:, i0 + 2:i1 + 2], ptb.t[:, i0 - 2:i1 - 2])))
                if w >= 16:
                    levels.append((8, Wd - 7, lambda i0, i1: (pta.t[:, i0 + 4:i1 + 4], pta.t[:, i0 - 4:i1 - 4])))
                bufs = [pta, ptb, pta, ptb]
                for li, (lo, hi, srcf) in enumerate(levels):
                    lastlev = (li == len(levels) - 1)
                    rd = [usrcb] if li == 0 else [bufs[li - 1].b]
                    if not lastlev:
                        dst = bufs[li]
                        a0, a1 = srcf(lo, hi)
                        S.op("pool", lambda e, dst=dst, lo=lo, hi=hi, a0=a0, a1=a1: e.tensor_tensor(
                            out=dst.t[:, lo:hi], in0=a0, in1=a1, op=ALU.add), reads=rd, writes=[dst.b])
                    else:
                        a0, a1 = srcf(8, 8 + Tn)
                        S.op("pool", lambda e, pg=pg, a0=a0, a1=a1: e.tensor_tensor(
                            out=dT.t[:, pg, :Tn], in0=a0, in1=a1, op=ALU.add), reads=rd, writes=[dT.cb[pg]])
                        for edge, on, c0 in ((0, first, 8), (1, lastg, 8 + Tn - 8)):
                            if on:
                                e0, e1 = srcf(c0, c0 + 8)
                                S.op("pool", lambda e, e0=e0, e1=e1: e.tensor_tensor(out=edg.t[:, :], in0=e0, in1=e1, op=ALU.add),
                                     reads=rd, writes=[edg.b])
                                S.op("pool", lambda e, edge=edge, pg=pg, c0=c0: e.tensor_tensor(
                                    out=dT.t[:, pg, c0 - 8:c0], in0=edg.t[:, :], in1=invc.t[:, edge, pg, :], op=ALU.mult),
                                    reads=[edg.b, invc.b], writes=[dT.cb[pg]])
            if cast_u:
                S.op("pool", lambda e: e.tensor_copy(out=dU.t[:, :, :Tn], in_=usrc[:, :, 8:8 + Tn]), reads=[usrcb], writes=[dU.b])

        def pool_proj(Tn):
            for pg in range(4):
                ps, psb = rot()

                def fn(e, pg=pg, ps=ps):
                    e.matmul(ps[:, :Tn], lhsT=poolTs.t[:, pg, :], rhs=dT.t[:, pg, :Tn], start=True, stop=False)
                    return e.matmul(ps[:, :Tn], lhsT=poolT.t[:, pg, :], rhs=dU.t[:, pg, :Tn], start=False, stop=True)
                S.op("pe", fn, reads=[poolT.b, poolTs.b, dT.cb[pg], dU.b], writes=[psb])
                S.op("dve", lambda e, pg=pg, ps=ps: e.tensor_scalar(out=catT.t[:, pg, :Tn], in0=ps[:, :Tn],
                                                                    scalar1=pscale.t[:, pg:pg + 1], scalar2=None, op0=ALU.mult),
                     reads=[psb, pscale.b], writes=[(catT.b,)])

        pend = []

        def flush_pending():
            while pend:
                pend.pop(0)()

        def attention(qsrc, q0, keys, nl, biasf, c0, rdb, fillers=()):
            nk = len(keys)

            def emit_S(h):
                c = h // 2
                pb = 64 * (h % 2)
                P = psS[h % 2]

                def fnS(e):
                    for i, (kt, kc0, Vt, vi) in enumerate(keys):
                        r = e.matmul(P.t[:, i * 128:(i + 1) * 128], lhsT=kt.t[pb:pb + 64, c, kc0:kc0 + 128],
                                     rhs=qsrc.t[pb:pb + 64, c, q0:q0 + 128], start=True, stop=True)
                    return r
                S.op("pe", fnS, reads=[qT.b, kT.b, kcT.b, qcT.b], writes=[BK[2 * (h % 2)], BK[2 * (h % 2) + 1]])

            def emit_soft(h):
                sbi = h % 2
                P = psS[sbi]
                Pb = [BK[2 * sbi], BK[2 * sbi + 1]]
                pt = pT[h % 3]
                S.op("act", lambda e: e.activation(out=pt.t[:, 0:nk * 128], in_=P.t[:, 0:nk * 128], func=AF.Exp),
                     reads=Pb, writes=[pt.b])
                if nl:
                    S.op("dve", lambda e: e.tensor_tensor(
                        out=pt.t[:, 0:nl * 128].rearrange("p (a q) -> p a q", q=128),
                        in0=pt.t[:, 0:nl * 128].rearrange("p (a q) -> p a q", q=128), in1=biasf(h), op=ALU.mult),
                        reads=[pt.b, rdb], writes=[pt.b])

            def emit_PV(h):
                pt = pT[h % 3]
                hh = h % 4

                def fnPV(e):
                    for i, (kt, kc0, Vt, vi) in enumerate(keys):
                        r = e.matmul(psO[h // 4].t[:, hh * 65:hh * 65 + 65], lhsT=pt.t[:, i * 128:(i + 1) * 128],
                                     rhs=Vt.t[:, vi, h, :], start=(i == 0), stop=(i == nk - 1))
                    return r
                S.op("pe", fnPV, reads=[pt.b, Vaug.b, Vcaug.b], writes=[BK[4 + h // 4]] if h % 4 == 0 else [(BK[4 + h // 4],)])

            fillers = list(fillers)
            emit_S(0)
            emit_soft(0)
            emit_S(1)
            emit_soft(1)
            flush_pending()
            for h in range(2, 8):
                emit_S(h)
                emit_PV(h - 2)
                emit_soft(h)
                if fillers:
                    fillers.pop(0)()
            emit_PV(6)
            emit_PV(7)
            while fillers:
                fillers.pop(0)()
            for hf in range(2):
                S.op("dve", lambda e, hf=hf: e.reciprocal(
                    out=rinv.t[:, hf * 4:(hf + 1) * 4],
                    in_=psO[hf].t[:, 0:260].rearrange("p (h d) -> p h d", d=65)[:, :, 64]),
                    reads=[BK[4 + hf]], writes=[(rinv.b,)])
            for hf in range(2):
                S.op("dve", lambda e, hf=hf: e.tensor_tensor(
                    out=On.t[:, hf * 256:(hf + 1) * 256].rearrange("p (h d) -> p h d", d=64),
                    in0=psO[hf].t[:, 0:260].rearrange("p (h d) -> p h d", d=65)[:, :, 0:64],
                    in1=rinv.t[:, hf * 4:(hf + 1) * 4].unsqueeze(2).broadcast_to([128, 4, 64]),
                    op=ALU.mult), reads=[BK[4 + hf], rinv.b], writes=[(On.b,)])

            def tail():
                def fnT(e):
                    for c in range(4):
                        r = e.transpose(out=psT.t[:, c * 128:(c + 1) * 128], in_=On.t[:, c * 128:(c + 1) * 128],
                                        identity=ident.t[:])
                    return r
                S.op("pe", fnT, reads=[On.b, ident.b], writes=[psT.b])
                S.op("dve", lambda e: e.tensor_copy(out=catT.t[:, 4:8, c0:c0 + 128],
                                                    in_=psT.t[:, 0:512].rearrange("p (c q) -> p c q", q=128)),
                     reads=[psT.b], writes=[(catT.b,)])
            pend.append(tail)

        ringO = Ring(wop, [m for g in range(NG + (0 if last else 1)) for m in range(8)], lambda m, wp, ex: S.dma(
            "act", out=wp.t[:].rearrange("p k c -> p (k c)"), in_=W["wout"][m], reads=[W["b_wout"]], writes=[wp.b]))
        ringO.prime()

        def outproj(Tn, j, g, G1=G1):
            accs = []
            for m in range(8):
                wp = ringO.get()
                ps, psb = rot()

                def fn(e, ps=ps, wp=wp):
                    for kc in range(8):
                        r = e.matmul(ps[:, :Tn], lhsT=wp.t[:, kc, :], rhs=catT.t[:, kc, :Tn],
                                     start=(kc == 0), stop=(kc == 7))
                    return r
                S.op("pe", fn, reads=[wp.b, catT.b], writes=[psb])
                ringO.done()
                S.op("dve", lambda e, m=m, ps=ps: e.tensor_copy(out=mixg.t[:, m, :Tn], in_=ps[:, :Tn]),
                     reads=[psb], writes=[mixg.cb[m]])
                if accs:
                    accs.pop(0)()
                accs.append(stats_sq(m, mixg.t[:, m, :Tn], mixg.cb[m], Tn, eng="dve"))
            accs.pop(0)()
            stats_finish(Tn)
            if g is None:
                residual_update(mixg, G1, j, ctxT.t, ctxT.b, Tn, ctxT.t, ctxT.b, tmps=xr[:2])
                return None

            def load_xr(m):
                S.dma("sp", out=xr[m % 3].t[:], in_=x_src[m * 128:(m + 1) * 128, g * T:(g + 1) * T], reads=[B_xs[g]],
                      writes=[xr[m % 3].b])

            def chunk(m):
                if m == 0:
                    for mm in range(3):
                        load_xr(mm)
                xc = xr[m % 3]
                S.op("dve", lambda e: e.scalar_tensor_tensor(
                    out=mixg.t[:, m, :Tn], in0=mixg.t[:, m, :Tn], scalar=G1.t[:, m, j:j + 1], in1=rstd.t[:, :Tn],
                    op0=ALU.mult, op1=ALU.mult), reads=[mixg.cb[m], G1.b, rstd.b], writes=[mixg.cb[m]])
                S.op("pool", lambda e: e.tensor_tensor(out=xc.t[:, :Tn], in0=mixg.t[:, m, :Tn], in1=xc.t[:, :Tn],
                                                       op=ALU.add), reads=[mixg.cb[m], xc.b], writes=[xc.b])
                S.dma("sp", out=xs[m * 128:(m + 1) * 128, g * T:(g + 1) * T], in_=xc.t[:], reads=[xc.b],
                      writes=[(B_xs[g],)])
                if m + 3 < 8:
                    load_xr(m + 3)
            return [(lambda m=m: chunk(m)) for m in range(8)]

        ckeys = [(kcT, i * 128, Vcaug, i) for i in range(CTX // 128)]
        kv_b = [qT.b, kT.b, Vaug.b, kcT.b, Vcaug.b, qcT.b]
        def load_ug(g):
            S.dma("sp", out=ug0.t[:], in_=uT[:, :, g * T:g * T + T + 16].rearrange("g p c -> p g c"),
                  reads=[B_uT], writes=[ug0.b])

        def load_bnd(jq):
            bj = {0: 0, 1: 1, 30: 2, 31: 3}[jq]
            S.dma("sp", out=bias_bnd.t[:].rearrange("p a h q -> p (a h q)"),
                  in_=W["bias"][:, (5 + 4 * bj) * 1024:(9 + 4 * bj) * 1024], reads=[W["b_bias"]], writes=[bias_bnd.b])
            S.op("act", lambda e: e.activation(out=bias_bnd.t[:].rearrange("p a h q -> p (a h q)"),
                                               in_=bias_bnd.t[:].rearrange("p a h q -> p (a h q)"), func=AF.Exp),
                 reads=[bias_bnd.b], writes=[bias_bnd.b])

        def load_dU(g):
            S.dma("sp", out=dU.t[:], in_=uTb[:, :, g * T:(g + 1) * T].rearrange("g p c -> p g c"), reads=[B_uTb],
                  writes=[dU.b])

        resid_pending = []
        load_bnd(0)
        load_ug(0)
        load_dU(0)
        poolmix(ug0.t, ug0.b, T, True, False, False)
        for g in range(NG):
            order = {0: [0, 2, 3, 1], NG - 1: [2, 0, 1, 3]}.get(g, [0, 1, 2, 3])
            for jj in order:
                jq = g * 4 + jj
                if 2 <= jq <= 29:
                    keys = [(kT, (jq + dt_) * 128, Vaug, jq + dt_) for dt_ in range(-2, 3)]
                    biasf = lambda h: bias_int.t[:, :, h, :]
                    rdb = bias_int.b
                else:
                    t0 = 0 if jq < 2 else 28
                    keys = [(kT, (t0 + i) * 128, Vaug, t0 + i) for i in range(4)]
                    biasf = lambda h: bias_bnd.t[:, :, h, :]
                    rdb = bias_bnd.b
                fl = resid_pending.pop(0) if (jj == order[1] and resid_pending) else ()
                attention(qT, jq * 128, keys + ckeys, len(keys), biasf, jj * 128, rdb, fillers=fl)
                tick(1)
                nxt_b = {0: 1, 1: 30, 30: 31}.get(jq)
                if nxt_b is not None:
                    load_bnd(nxt_b)
                if jq == 0:
                    load_ug(1)
            flush_pending()
            if g == NG - 1:
                need(l, "w1")
                need(l, "w2")
                ring2.prime()
                if last:
                    ring1.prime()
            pool_proj(T)
            resid_pending.append(outproj(T, 0, g))
            if g + 1 < NG:
                load_dU(g + 1)
                poolmix(ug0.t, ug0.b, T, False, g + 1 == NG - 1, False)
            if g + 2 < NG:
                load_ug(g + 2)
        lastres = resid_pending.pop(0)
        if last:
            for f_ in lastres:
                f_()
        if not last:
            poolmix(ucT.t, ucT.b, CTX, True, True, True)
            for jj in range(CTX // 128):
                fl = lastres[:6] if jj == 0 else lastres[6:]
                attention(qcT, jj * 128, ckeys, 0, None, jj * 128, None, fillers=fl)
            flush_pending()
            ring1.prime()
            pool_proj(CTX)
            outproj(CTX, 1, None)
        S.barrier()

        ring2.prime_rest()
        ring1.prime_rest()
        xgC = [xg, xgC2]
        hTC = [hT, hTC2]

        def loadC(g):
            if g == NG:
                src = ctxT
            else:
                src = xgC[g % 2]
                S.dma("sp", out=src.t[:], in_=xs[:, g * T:(g + 1) * T].rearrange("(k p) t -> p k t", p=128),
                      reads=[B_xs[g]], writes=[src.b])
            Tn = CTX if g == NG else T
            for k in range(8):
                S.op("pool" if k % 2 == 0 else "dve", lambda e, k=k, src=src, Tn=Tn: e.tensor_tensor(
                    out=sqC.t[:, k, :Tn], in0=src.t[:, k, :Tn], in1=src.t[:, k, :Tn], op=ALU.mult),
                    reads=[src.b], writes=[sqC.cb[k]])
            return src

        def restC(g, src):
            Tn = CTX if g == NG else T
            j = 1 if g == NG else 0

            def fn(e):
                for k in range(8):
                    r = e.matmul(psX.t[:, :Tn], lhsT=ones.t[:], rhs=sqC.t[:, k, :Tn], start=(k == 0), stop=(k == 7))
                return r
            S.op("pe", fn, reads=sqC.cb + [ones.b], writes=[psX.b])
            stats_finish(Tn)
            modulate(src.t, src.b, Tn, A2, lambda k, j=j, mod4=mod4: mod4[:, 3, k, j:j + 1], j, hT=hTC[g % 2])
            return src

        def prologueC(g):
            return restC(g, loadC(g))

        srcT = prologueC(0)
        for g in range(ngc):
            isc = (g == NG)
            Tn = CTX if isc else T
            hTg = hTC[g % 2]
            cur = srcT
            nxtT = None
            for jb in range(8):
                if jb == 4 and g + 1 < ngc:
                    nxtT = loadC(g + 1)
                wp = ring1.get()
                for jj in range(4):
                    ps, psb = rot()

                    def fn(e, jj=jj, ps=ps, wp=wp, Tn=Tn, hTg=hTg):
                        for k in range(8):
                            r = e.matmul(ps[:, :Tn], lhsT=wp.t[:, k, jj * 128:(jj + 1) * 128], rhs=hTg.t[:, k, :Tn],
                                         start=(k == 0), stop=(k == 7))
                        return r
                    S.op("pe", fn, reads=[wp.b] + hTg.cb, writes=[psb])
                    tf = rtmp[jj % 4]
                    S.op("act", lambda e, ps=ps, tf=tf, Tn=Tn: e.activation(out=tf.t[:, :Tn], in_=ps[:, :Tn], func=AF.Relu),
                         reads=[psb], writes=[tf.b])
                    S.op("pool" if jj % 2 else "dve",
                         lambda e, tf=tf, jb=jb, jj=jj, Tn=Tn: e.tensor_tensor(out=actT.t[:, jb * 4 + jj, :Tn], in0=tf.t[:, :Tn],
                                                                             in1=tf.t[:, :Tn], op=ALU.mult),
                         reads=[tf.b], writes=[(actT.b,)])
                ring1.done()
            if g + 1 < ngc:
                srcT = restC(g + 1, nxtT)
            elif l + 1 < depth:
                phaseM(l + 1)
            accs = []
            for m in range(8):
                w2p = ring2.get()
                ps, psb = rot()

                def fn(e, ps=ps, w2p=w2p, Tn=Tn):
                    for jx in range(32):
                        r = e.matmul(ps[:, :Tn], lhsT=w2p.t[:, jx, :], rhs=actT.t[:, jx, :Tn], start=(jx == 0), stop=(jx == 31))
                    return r
                S.op("pe", fn, reads=[w2p.b, actT.b], writes=[psb])
                ring2.done()
                if accs:
                    accs.pop(0)()
                S.op("act", lambda e, m=m, ps=ps, Tn=Tn: e.activation(out=mixg.t[:, m, :Tn], in_=ps[:, :Tn], func=AF.Identity),
                     reads=[psb], writes=[mixg.cb[m]])
                accs.append(stats_sq(m, mixg.t[:, m, :Tn], mixg.cb[m], Tn))
            accs.pop(0)()
            stats_finish(Tn)
            if isc:
                residual_update(mixg, G2, 1, ctxT.t, ctxT.b, Tn, ctxT.t, ctxT.b)
            else:
                residual_update(mixg, G2, 0, cur.t, cur.b, Tn, cur.t, cur.b)
                dst = yT if last else xs
                S.dma("sp", out=dst[:, g * T:(g + 1) * T].rearrange("(k p) t -> p k t", p=128), in_=cur.t[:],
                      reads=[cur.b], writes=[B_xs[g]])
            tick(2)
        S.barrier()

    with nc.Block() as block:
        @block.sync
        def _(e):
            for th in S.ops["sp"]:
                th(e)

        @block.tensor
        def _(e):
            for th in S.ops["pe"]:
                th(e)

        @block.scalar
        def _(e):
            for th in S.ops["act"]:
                th(e)

        @block.vector
        def _(e):
            for th in S.ops["dve"]:
                th(e)

        @block.gpsimd
        def _(e):
            for th in S.ops["pool"]:
                th(e)
    return nc


def _bias_tiles(rpb):
    depth = rpb.shape[0]
    kl = np.arange(128) // 64
    kc = np.arange(128) % 64
    rl = np.arange(128) // 64
    c = np.arange(128) % 64
    cs = np.clip(c - 8, 0, GRID_W - 16)
    out = np.empty((depth, 128, 21, 8, 128), np.float32)

    def tile(j, t):
        kr = 2 * t + kl
        r = 2 * j + rl
        rs = np.clip(r - 4, 0, 64 - 8)
        okr = (kr[:, None] >= rs[None, :]) & (kr[:, None] < rs[None, :] + 8)
        okc = (kc[:, None] >= cs[None, :]) & (kc[:, None] < cs[None, :] + 16)
        dr = np.clip(kr[:, None] - r[None, :] + 7, 0, 14)
        dc = np.clip(kc[:, None] - c[None, :] + 15, 0, 30)
        vals = rpb[:, :, dr, dc]
        vals = np.where((okr & okc)[None, None], vals, np.float32(NEG))
        return np.transpose(vals, (0, 2, 1, 3))

    for i, dt_ in enumerate(range(-2, 3)):
        out[:, :, i] = tile(10, 10 + dt_)
    for bj, j in enumerate((0, 1, 30, 31)):
        t0 = 0 if j < 2 else 28
        for i in range(4):
            out[:, :, 5 + 4 * bj + i] = tile(j, t0 + i)
    return out.reshape(depth, 128, 21 * 8 * 128)


def _invc():
    out = np.ones((2, 4, 8), np.float32)
    Lh = 1024
    for pg, w in enumerate((2, 4, 8, 16)):
        lo = w // 2
        hi = w - 1 - lo
        for i in range(8):
            t = i
            cnt = min(t + hi + 1, Lh) - max(t - lo, 0)
            out[0, pg, i] = np.float32(w) / np.float32(cnt)
            t = Lh - 8 + i
            cnt = min(t + hi + 1, Lh) - max(t - lo, 0)
            out[1, pg, i] = np.float32(w) / np.float32(cnt)
    return np.broadcast_to(out.reshape(1, 64), (128, 64)).copy()


def prep_shared(depth, w_mod, b_mod, g_pre_mix, g_post_mix, g_pre_mlp, g_post_mlp, w_in, pool_w, pool_scale, rpb,
                w_out, w_mlp_in, w_mlp_out):
    f = np.float32

    def pk(v, n):
        return np.ascontiguousarray(np.transpose(v.reshape(depth, n, 128), (0, 2, 1)))
    bm = pk(np.asarray(b_mod[:depth], f), 48)
    gv = np.concatenate([pk(np.asarray(a[:depth], f), 8) for a in (g_pre_mix, g_post_mix, g_pre_mlp, g_post_mlp)], axis=2)
    return {
        "ident": np.eye(128, dtype=f),
        "invc": _invc(),
        "w_mod": np.ascontiguousarray(w_mod[:depth], f),
        "b_modT": np.ascontiguousarray(np.repeat(bm, 2, axis=2)),
        "gvec": np.ascontiguousarray(gv),
        "pscale": pk(np.asarray(pool_scale[:depth], f), 4),
        "w_in": np.ascontiguousarray(w_in[:depth], f),
        "pool_w": np.ascontiguousarray(np.asarray(pool_w[:depth], f).reshape(depth, 512, 128)),
        "biasT": _bias_tiles(np.asarray(rpb[:depth], f)),
        "w_out": np.ascontiguousarray(w_out[:depth], f),
        "w1": np.ascontiguousarray(w_mlp_in[:depth], f),
        "w2": np.ascontiguousarray(w_mlp_out[:depth], f),
    }


def prep_core(x_b, c_b, ctx_b, c_ctx):
    f = np.float32
    cc = np.stack([np.asarray(c_b, f), np.asarray(c_ctx, f)], axis=1)
    cc = np.transpose(cc.reshape(8, 128, 2), (1, 0, 2)).reshape(128, 16)
    return {
        "xT": np.ascontiguousarray(np.asarray(x_b, f).T),
        "ctxT": np.ascontiguousarray(np.asarray(ctx_b, f).T),
        "cc": np.ascontiguousarray(cc),
    }


_NC_CACHE = {}


def kernel(x, c, ctx, c_ctx, w_mod, b_mod, g_pre_mix, g_post_mix, g_pre_mlp, g_post_mlp,
           w_in, pool_w, pool_scale, rpb, w_out, w_mlp_in, w_mlp_out):
    x = np.asarray(x)
    nb = x.shape[0]
    shared = prep_shared(DEPTH, np.asarray(w_mod), np.asarray(b_mod), np.asarray(g_pre_mix), np.asarray(g_post_mix),
                         np.asarray(g_pre_mlp), np.asarray(g_post_mlp), np.asarray(w_in), np.asarray(pool_w),
                         np.asarray(pool_scale), np.asarray(rpb), np.asarray(w_out), np.asarray(w_mlp_in),
                         np.asarray(w_mlp_out))
    in_maps = []
    for b in range(nb):
        m = dict(shared)
        m.update(prep_core(x[b], np.asarray(c)[b], np.asarray(ctx)[b], np.asarray(c_ctx)))
        in_maps.append(m)
    nc = build(DEPTH)
    res = run_bass_kernel_spmd(nc, in_maps, core_ids=list(range(nb)))
    out = np.stack([np.ascontiguousarray(r["yT"].T) for r in res.results], axis=0)
    return out.astype(np.float32)
```

```python
import numpy as np
import concourse.bass as bass
import concourse.mybir as mybir
from concourse.bass_utils import run_bass_kernel_spmd

F32 = mybir.dt.float32
BF16 = mybir.dt.bfloat16
AF = mybir.ActivationFunctionType
ALU = mybir.AluOpType

D = 1024
L = 4096
CTX = 256
DEPTH = 4
DFF = 4096
NH = 8
T = 512
NG = L // T
EPS = 1e-6
NEG = -30000.0
NDS = 26
GRID_W = 64


class Buf:
    __slots__ = ("name", "w", "r")

    def __init__(self, name):
        self.name = name
        self.w = {}
        self.r = {}


class Tl:
    def __init__(self, nc, name, shape, dtype, psum=False, off=None):
        if psum:
            self.t = nc.alloc_psum_tensor(name, shape, dtype)
        else:
            self.t = nc.alloc_sbuf_tensor_at(name, shape, dtype, offset=off)
        self.b = Buf(name)


class Sched:
    def __init__(self, nc):
        self.nc = nc
        self.E = {"pe": nc.tensor, "act": nc.scalar, "dve": nc.vector, "pool": nc.gpsimd, "sp": nc.sync}
        self.sem = {e: nc.alloc_semaphore("sem_" + e) for e in ("pe", "act", "dve", "pool")}
        self.cnt = {e: 0 for e in self.sem}
        self.ops = {e: [] for e in self.E}
        self.waited = {e: {} for e in self.E}
        self.dsem = [nc.alloc_semaphore("dsem%d" % i) for i in range(NDS)]
        self.dcnt = [0] * NDS
        self.dpool = {"sp": list(range(0, 16)), "io": [], "cast": list(range(16, 22)), "actq": list(range(22, 26))}
        self.dnext = {"sp": 0, "io": 0, "cast": 0, "actq": 0}

    def _semh(self, key):
        if isinstance(key, tuple):
            return self.dsem[key[1]]
        return self.sem[key]

    def _deps(self, reads, writes):
        evs = []
        for b in reads:
            evs += list(b.w.items())
        for b in writes:
            if isinstance(b, tuple):
                evs += list(b[0].r.items())
            else:
                evs += list(b.w.items()) + list(b.r.items())
        return evs

    def _waits(self, eng, evs):
        out = []
        w = self.waited[eng]
        for key, val in evs:
            if key == eng and eng == "pe":
                continue
            if w.get(key, 0) >= val:
                continue
            w[key] = val
            out.append((key, val))
        return out

    def _update(self, ev, reads, writes):
        k, v = ev
        for b in reads:
            b.r[k] = v
        for b in writes:
            if isinstance(b, tuple):
                b[0].w[k] = v
            else:
                b.w = {k: v}
                b.r = {}

    def op(self, eng, fn, reads=(), writes=()):
        waits = self._waits(eng, self._deps(reads, writes))
        self.cnt[eng] += 1
        ev = (eng, self.cnt[eng])
        sem = self.sem[eng]

        def thunk(e):
            for k, v in waits:
                e.wait_ge(self._semh(k), v)
            fn(e).then_inc(sem, 1)

        self.ops[eng].append(thunk)
        self._update(ev, reads, writes)

    def dma(self, q, out, in_, reads=(), writes=(), cls=None):
        cls = cls or {"sp": "sp", "pool": "cast", "act": "actq"}[q]
        pl = self.dpool[cls]
        i = pl[self.dnext[cls] % len(pl)]
        self.dnext[cls] += 1
        evs = self._deps(reads, writes)
        if self.dcnt[i]:
            evs.append((("d", i), self.dcnt[i]))
        waits = self._waits(q, evs)
        self.dcnt[i] += 16
        ev = (("d", i), self.dcnt[i])
        ds = self.dsem[i]

        def thunk(e):
            for k, v in waits:
                e.wait_ge(self._semh(k), v)
            e.dma_start(out=out, in_=in_).then_inc(ds, 16)

        self.ops[q].append(thunk)
        self._update(ev, reads, writes)

    def barrier(self):
        evs = [(e, c) for e, c in self.cnt.items() if c] + [(("d", i), self.dcnt[i]) for i in self.dpool["sp"] + self.dpool["actq"] if self.dcnt[i]]
        for eng in self.E:
            waits = self._waits(eng, [ev for ev in evs if ev[0] != eng])

            def thunk(e, waits=waits):
                for k, v in waits:
                    e.wait_ge(self._semh(k), v)

            self.ops[eng].append(thunk)


class Ring:
    def __init__(self, bufs, items, loader, alias=()):
        self.bufs = bufs
        self.items = items
        self.loader = loader
        self.alias = list(alias)
        self.nl = 0
        self.nu = 0

    def _load(self, extra=()):
        if self.nl < len(self.items):
            self.loader(self.items[self.nl], self.bufs[self.nl % len(self.bufs)], list(extra))
            self.nl += 1

    def prime(self):
        for _ in self.bufs:
            self._load([(b,) for b in self.alias])

    def prime_rest(self):
        while self.nl < len(self.bufs):
            self._load()

    def get(self):
        b = self.bufs[self.nu % len(self.bufs)]
        self.nu += 1
        return b

    def done(self):
        self._load()


def build(depth=DEPTH):
    nc = bass.Bass("TRN2", target_bir_lowering=False)
    S = Sched(nc)

    def dram(name, shape, dtype, kind):
        return nc.dram_tensor(name, shape, dtype, kind=kind).ap()

    xT_in = dram("xT", [D, L], F32, "ExternalInput")
    ctxT_in = dram("ctxT", [D, CTX], F32, "ExternalInput")
    cc_in = dram("cc", [128, 16], F32, "ExternalInput")
    ident_in = dram("ident", [128, 128], F32, "ExternalInput")
    invc_in = dram("invc", [128, 64], F32, "ExternalInput")
    wmod_in = dram("w_mod", [depth, D, 6 * D], F32, "ExternalInput")
    bmod_in = dram("b_modT", [depth, 128, 96], F32, "ExternalInput")
    gvec_in = dram("gvec", [depth, 128, 32], F32, "ExternalInput")
    pscale_in = dram("pscale", [depth, 128, 4], F32, "ExternalInput")
    win_in = dram("w_in", [depth, D, 2048], F32, "ExternalInput")
    poolw_in = dram("pool_w", [depth, 512, 128], F32, "ExternalInput")
    bias_in = dram("biasT", [depth, 128, 21 * 8 * 128], F32, "ExternalInput")
    wout_in = dram("w_out", [depth, D, D], F32, "ExternalInput")
    w1_in = dram("w1", [depth, D, DFF], F32, "ExternalInput")
    w2_in = dram("w2", [depth, DFF, D], F32, "ExternalInput")
    yT = dram("yT", [D, L], F32, "ExternalOutput")

    xs = dram("xs", [D, L], F32, "Internal")
    B_xs = [Buf("xs%d" % g) for g in range(NG)]
    uT = dram("uT", [4, 128, L + 16], F32, "Internal")
    B_uT = Buf("uT")
    uTb = dram("uTb", [4, 128, L], BF16, "Internal")
    B_uTb = Buf("uTb")
    WB = []
    for l in range(depth):
        WB.append(dict(
            wmod=dram("wmod_b%d" % l, [D, 6 * D], BF16, "Internal"), b_wmod=Buf("wmod"),
            win=dram("win_b%d" % l, [D, 2048], BF16, "Internal"), b_win=Buf("win"),
            wout=dram("wout_b%d" % l, [8, 128, 8 * 128], BF16, "Internal"), b_wout=Buf("wout"),
            w1=dram("w1_b%d" % l, [D, DFF], BF16, "Internal"), b_w1=Buf("w1"),
            w2=dram("w2_b%d" % l, [8, 128, 32 * 128], BF16, "Internal"), b_w2=Buf("w2"),
            pool=dram("pool_b%d" % l, [512, 128], BF16, "Internal"), b_pool=Buf("pool"),
            bias=dram("bias_b%d" % l, [128, 21 * 8 * 128], BF16, "Internal"), b_bias=Buf("bias"),
        ))

    ptr = [16512]

    def _sz(shape, dtype):
        n = 1
        for d_ in shape[1:]:
            n *= d_
        return n * (2 if dtype == BF16 else 4)

    def sb(name, shape, dtype, off=None):
        if off is None:
            off = ptr[0]
            ptr[0] = (off + _sz(shape, dtype) + 63) // 64 * 64
        return Tl(nc, name, shape, dtype, off=off)

    def union(nbytes):
        off = ptr[0]
        ptr[0] = (off + nbytes + 63) // 64 * 64
        return off

    def sub(base, items):
        out = []
        o = base
        for name, shape, dtype in items:
            out.append(Tl(nc, name, shape, dtype, off=o))
            o = (o + _sz(shape, dtype) + 63) // 64 * 64
        return out, o - base

    ident = sb("ident", [128, 128], BF16)
    ones = sb("ones", [128, 128], BF16)
    epsc = sb("epsc", [128, 1], F32)
    invc = sb("invc_t", [128, 2, 4, 8], F32)
    cc = sb("cc_t", [128, 16], F32)
    sT = sb("sT", [128, 16], BF16)
    bmod = sb("bmod", [128, 96], F32)
    gvec = sb("gvec_t", [128, 32], F32)
    pscale = sb("pscale_t", [128, 4], F32)
    modT = sb("modT", [128, 96], F32)
    A1 = sb("A1", [128, 8, 2], F32)
    G1 = sb("G1", [128, 8, 2], F32)
    A2 = sb("A2", [128, 8, 2], F32)
    G2 = sb("G2", [128, 8, 2], F32)
    modT_b = sb("modT_b", [128, 96], F32)
    A1_b = sb("A1_b", [128, 8, 2], F32)
    G1_b = sb("G1_b", [128, 8, 2], F32)
    A2_b = sb("A2_b", [128, 8, 2], F32)
    G2_b = sb("G2_b", [128, 8, 2], F32)
    ctxT = sb("ctxT_t", [128, 8, CTX], F32)
    sq2 = [sb("sq2_%d" % i, [128, T], BF16) for i in range(3)]
    rstd = sb("rstd", [128, T], F32)
    tmpf_off = ptr[0]
    tmpf = [sb("tmpf%d" % i, [128, T], F32) for i in range(2)]
    umix = union(16384)
    mixg = Tl(nc, "mixg", [128, 8, T], F32, off=umix)
    mixg.cb = [Buf("mixg%d" % i) for i in range(8)]
    xgA2 = Tl(nc, "xgA2", [128, 8, T], F32, off=umix)
    qcT = sb("qcT", [128, 4, CTX], BF16)
    kcT = sb("kcT", [128, 4, CTX], BF16)
    Vcaug = sb("Vcaug", [128, CTX // 128, 8, 65], BF16)
    ucT_off = ptr[0]
    ucT = sb("ucT", [128, 4, CTX + 16], F32)
    rtmp = tmpf + [Tl(nc, "rtmp2", [128, T], F32, off=ucT_off), Tl(nc, "rtmp3", [128, T], F32, off=ucT_off + 2048)]
    rinv = sb("rinv", [128, 8], F32)
    On = sb("On", [128, 512], BF16)
    poolT = sb("poolT", [128, 4, 128], BF16)
    dT = sb("dT", [128, 4, T], BF16)
    dT.cb = [Buf("dT%d" % i) for i in range(4)]
    dU = Tl(nc, "dU", [128, 4, T], BF16, off=tmpf_off)
    poolTs = sb("poolTs", [128, 4, 128], BF16)
    edg = sb("edg", [128, 8], F32)
    u5 = union(33280)
    Vaug = Tl(nc, "Vaug", [128, L // 128, 8, 65], BF16, off=u5)
    (xgC2, hTC2), n_ = sub(u5, [("xgC2", [128, 8, T], F32), ("hTC2", [128, 8, T], BF16)])
    hTC2.cb = [Buf("hTC2_%d" % i) for i in range(8)]
    assert n_ <= 33280
    ux = union(16384)
    xg = Tl(nc, "xg", [128, 8, T], F32, off=ux)
    (xr0, xr1, xr2, ug0), n_ = sub(ux, [("xr0", [128, T], F32), ("xr1", [128, T], F32), ("xr2", [128, T], F32),
                                        ("ug0", [128, 4, T + 16], F32)])
    assert n_ <= 16384
    xr = [xr0, xr1, xr2]
    u1 = union(8192)
    hT = Tl(nc, "hT", [128, 8, T], BF16, off=u1)
    hT_default = hT
    hT.cb = [Buf("hT_%d" % i) for i in range(8)]
    catT = Tl(nc, "catT", [128, 8, T], BF16, off=u1)
    u2 = union(24576)
    wpan, n_ = sub(u2, [("wpan%d" % i, [128, 8, 512], BF16) for i in range(3)])
    (bias_int, bias_bnd, pT0, pT1, pT2), n_ = sub(u2, [("bias_int", [128, 5, 8, 128], BF16), ("bias_bnd", [128, 4, 8, 128], BF16),
                                                       ("pT0", [128, 7 * 128], BF16), ("pT1", [128, 7 * 128], BF16),
                                                       ("pT2", [128, 7 * 128], BF16)])
    assert n_ <= 24576, n_
    pT = [pT0, pT1, pT2]
    u3 = union(32768)
    qT = Tl(nc, "qT", [128, 4, L], BF16, off=u3)
    actT = Tl(nc, "actT", [128, 32, T], BF16, off=u3)
    u4 = union(32768)
    kT = Tl(nc, "kT", [128, 4, L], BF16, off=u4)
    w2pan, n_ = sub(u4, [("w2pan%d" % i, [128, 32, 128], BF16) for i in range(3)])
    wpan4 = Tl(nc, "wpan4", [128, 8, 512], BF16, off=u4 + 24576)
    u6 = union(10432)
    uev = Tl(nc, "uev", [128, 4, T], F32, off=u6)
    (pta, ptb, wop0, wop1, wop2), n_ = sub(u6, [("pta", [128, T + 16], F32), ("ptb", [128, T + 16], F32),
                                                ("wop0", [128, 8, 128], BF16), ("wop1", [128, 8, 128], BF16),
                                                ("wop2", [128, 8, 128], BF16)])
    assert n_ <= 10432, n_
    wop = [wop0, wop1, wop2]
    sqC = Tl(nc, "sqC", [128, 8, T], BF16, off=u6)
    sqC.cb = [Buf("sqC%d" % i) for i in range(8)]
    wpan5 = Tl(nc, "wpan5", [128, 8, 512], BF16, off=u5 + 24576)
    assert ptr[0] <= nc.SBUF_PARTITION_SIZE_BYTES, ptr[0]

    psS = [Tl(nc, "psS%d" % i, [128, 1024], F32, psum=True) for i in range(2)]
    psO = [Tl(nc, "psO%d" % i, [128, 512], F32, psum=True) for i in range(2)]
    psT = Tl(nc, "psT", [128, 1024], BF16, psum=True)
    psX = Tl(nc, "psX", [128, 512], F32, psum=True)
    BK = [Buf("bank%d" % i) for i in range(6)]
    ROT = [(psS[0].t[:, 0:512], BK[0]), (psS[0].t[:, 512:1024], BK[1]),
           (psS[1].t[:, 0:512], BK[2]), (psS[1].t[:, 512:1024], BK[3]),
           (psO[0].t[:, :], BK[4]), (psO[1].t[:, :], BK[5])]
    rot_i = [0]

    def rot():
        r = ROT[rot_i[0] % 6]
        rot_i[0] += 1
        return r

    S.dma("pool", out=ident.t[:], in_=ident_in, writes=[ident.b])
    S.dma("sp", out=invc.t[:].rearrange("p a b c -> p (a b c)"), in_=invc_in, writes=[invc.b])
    S.dma("sp", out=cc.t[:], in_=cc_in, writes=[cc.b])
    S.dma("sp", out=ctxT.t[:], in_=ctxT_in.rearrange("(k p) t -> p k t", p=128), writes=[ctxT.b])
    S.op("dve", lambda e: e.memset(ones.t[:], 1.0), writes=[ones.b])
    S.op("dve", lambda e: e.memset(epsc.t[:], EPS), writes=[epsc.b])
    S.op("dve", lambda e: e.memset(rstd.t[:, 0:32], 0.0), writes=[rstd.b])
    S.op("dve", lambda e: e.memset(Vaug.t[:], 1.0), writes=[Vaug.b])
    S.op("dve", lambda e: e.memset(Vcaug.t[:], 1.0), writes=[Vcaug.b])
    S.op("dve", lambda e: e.memset(ucT.t[:], 0.0), writes=[ucT.b])
    S.op("act", lambda e: e.activation(out=sT.t[:], in_=cc.t[:], func=AF.Silu), reads=[cc.b], writes=[sT.b])
    zsrc = rstd.t[:, 0:32].rearrange("p (g c) -> p g c", c=8)
    S.dma("sp", out=uT[:, :, 0:8].rearrange("g p c -> p g c"), in_=zsrc, reads=[rstd.b], writes=[(B_uT,)])
    S.dma("sp", out=uT[:, :, L + 8:L + 16].rearrange("g p c -> p g c"), in_=zsrc, reads=[rstd.b], writes=[(B_uT,)])

    pending = []

    def queue_casts(l):
        W = WB[l]

        def add(name, out, in_):
            pending.append((l, name, lambda: S.dma("pool", out=out, in_=in_, writes=[(W["b_" + name],)])))
        for i in range(4):
            add("win", W["win"][i * 256:(i + 1) * 256, :], win_in[l, i * 256:(i + 1) * 256, :])
        for i in range(16):
            add("wmod", W["wmod"][i * 64:(i + 1) * 64, :], wmod_in[l, i * 64:(i + 1) * 64, :])
        add("pool", W["pool"], poolw_in[l])
        for i in range(4):
            add("bias", W["bias"][i * 32:(i + 1) * 32, :], bias_in[l, i * 32:(i + 1) * 32, :])
        for m in range(8):
            add("wout", W["wout"][m].rearrange("p (k c) -> p k c", c=128),
                wout_in[l, :, m * 128:(m + 1) * 128].rearrange("(k p) c -> p k c", p=128))
        for i in range(8):
            add("w1", W["w1"][i * 128:(i + 1) * 128, :], w1_in[l, i * 128:(i + 1) * 128, :])
        for m in range(8):
            for jq in range(2):
                add("w2", W["w2"][m].rearrange("p (j c) -> p j c", c=128)[:, jq * 16:(jq + 1) * 16, :],
                    w2_in[l, jq * 2048:(jq + 1) * 2048, m * 128:(m + 1) * 128].rearrange("(j p) c -> p j c", p=128))

    def tick(k):
        for _ in range(k):
            if pending:
                pending.pop(0)[2]()

    def need(l, name):
        while any(p[0] == l and p[1] == name for p in pending):
            pending.pop(0)[2]()

    for l_ in range(depth):
        queue_casts(l_)

    sq_i = [0]

    def stats_sq(m, src_ap, srcb, Tn, eng="pool"):
        sq = sq2[sq_i[0] % 3]
        sq_i[0] += 1
        if eng == "dve":
            S.op("dve", lambda e: e.tensor_tensor(out=sq.t[:, :Tn], in0=src_ap, in1=src_ap, op=ALU.mult),
                 reads=[srcb], writes=[sq.b])
        else:
            S.op("pool", lambda e: e.tensor_tensor(out=sq.t[:, :Tn], in0=src_ap, in1=src_ap, op=ALU.mult),
                 reads=[srcb], writes=[sq.b])

        def acc():
            S.op("pe", lambda e: e.matmul(psX.t[:, :Tn], lhsT=ones.t[:], rhs=sq.t[:, :Tn], start=(m == 0), stop=(m == 7)),
                 reads=[sq.b, ones.b], writes=[psX.b] if m == 0 else [(psX.b,)])
        return acc

    def stats_sq_acc(m, src_ap, srcb, Tn):
        stats_sq(m, src_ap, srcb, Tn)()

    def stats_finish(Tn):
        S.op("act", lambda e: e.activation(out=rstd.t[:, :Tn], in_=psX.t[:, :Tn], func=AF.Ln, bias=epsc.t[:, 0:1],
                                           scale=1.0 / D), reads=[psX.b, epsc.b], writes=[rstd.b])
        S.op("act", lambda e: e.activation(out=rstd.t[:, :Tn], in_=rstd.t[:, :Tn], func=AF.Exp, scale=-0.5),
             reads=[rstd.b], writes=[rstd.b])

    def stats(src, Tn, srcb, alt=False):
        for k in range(8):
            stats_sq(k, src[:, k, :Tn], srcb, Tn, eng=("dve" if (alt and k % 2) else "pool"))()
        stats_finish(Tn)

    def modulate_chunk(k, src, srcb, Tn, A, Bm, j, hT):
        tf = tmpf[k % 2]
        S.op("dve", lambda e: e.tensor_tensor(out=tf.t[:, :Tn], in0=src[:, k, :Tn], in1=rstd.t[:, :Tn], op=ALU.mult),
             reads=[srcb, rstd.b], writes=[tf.b])
        S.op("act", lambda e: e.activation(out=hT.t[:, k, :Tn], in_=tf.t[:, :Tn], func=AF.Identity, bias=Bm(k),
                                           scale=A.t[:, k, j:j + 1]),
             reads=[tf.b, A.b, A.modb], writes=[hT.cb[k]])

    def modulate(src, srcb, Tn, A, Bm, j, hT=None):
        hT = hT or hT_default
        for k in range(8):
            modulate_chunk(k, src, srcb, Tn, A, Bm, j, hT)

    MODS = [(modT, A1, G1, A2, G2, modT.t[:].rearrange("p (w k j) -> p w k j", w=6, k=8, j=2)),
            (modT_b, A1_b, G1_b, A2_b, G2_b, modT_b.t[:].rearrange("p (w k j) -> p w k j", w=6, k=8, j=2))]

    def residual_update(src_y, G, j, xres, xresb, Tn, dst, dstb, tmps=None):
        for m in range(8):
            S.op("dve", lambda e, m=m: e.scalar_tensor_tensor(
                out=src_y.t[:, m, :Tn], in0=src_y.t[:, m, :Tn], scalar=G.t[:, m, j:j + 1], in1=rstd.t[:, :Tn],
                op0=ALU.mult, op1=ALU.mult), reads=[src_y.cb[m], G.b, rstd.b], writes=[src_y.cb[m]])
            S.op("pool", lambda e, m=m: e.tensor_tensor(out=dst[:, m, :Tn], in0=src_y.t[:, m, :Tn], in1=xres[:, m, :Tn],
                                                        op=ALU.add), reads=[src_y.cb[m], xresb], writes=[(dstb,)])

    def phaseM(l):
        need(l, "wmod")
        W = WB[l]
        modT, A1, G1, A2, G2, mod4 = MODS[l % 2]
        for a_ in (A1, G1, A2, G2):
            a_.modb = modT.b
        S.dma("sp", out=bmod.t[:], in_=bmod_in[l], writes=[bmod.b])
        S.dma("sp", out=gvec.t[:], in_=gvec_in[l], writes=[gvec.b])
        S.dma("sp", out=pscale.t[:], in_=pscale_in[l], writes=[pscale.b])
        ringM = Ring(wpan, list(range(12)), lambda piece, wm, ex: S.dma(
            "sp", out=wm.t[:], in_=W["wmod"][:, piece * 512:(piece + 1) * 512].rearrange("(k p) n -> p k n", p=128),
            reads=[W["b_wmod"]], writes=[wm.b]))
        ringM.prime()
        for piece in range(12):
            wm = ringM.get()

            def fn(e, piece=piece, wm=wm):
                for nn in range(4):
                    nch = piece * 4 + nn
                    for k in range(8):
                        r = e.matmul(psX.t[:, nch * 2:nch * 2 + 2], lhsT=wm.t[:, k, nn * 128:(nn + 1) * 128],
                                     rhs=sT.t[:, 2 * k:2 * k + 2], start=(k == 0), stop=(k == 7))
                return r
            S.op("pe", fn, reads=[wm.b, sT.b], writes=[psX.b] if piece == 0 else [(psX.b,)])
            ringM.done()
        S.op("dve", lambda e: e.tensor_tensor(out=modT.t[:], in0=psX.t[:, 0:96], in1=bmod.t[:], op=ALU.add),
             reads=[psX.b, bmod.b], writes=[modT.b])
        for j in range(2):
            S.op("dve", lambda e, j=j: e.scalar_tensor_tensor(out=A1.t[:, :, j], in0=mod4[:, 1, :, j], scalar=1.0,
                                                              in1=gvec.t[:, 0:8], op0=ALU.add, op1=ALU.mult),
                 reads=[modT.b, gvec.b], writes=[(A1.b,)])
            S.op("dve", lambda e, j=j: e.tensor_tensor(out=G1.t[:, :, j], in0=mod4[:, 2, :, j], in1=gvec.t[:, 8:16],
                                                       op=ALU.mult), reads=[modT.b, gvec.b], writes=[(G1.b,)])
            S.op("dve", lambda e, j=j: e.scalar_tensor_tensor(out=A2.t[:, :, j], in0=mod4[:, 4, :, j], scalar=1.0,
                                                              in1=gvec.t[:, 16:24], op0=ALU.add, op1=ALU.mult),
                 reads=[modT.b, gvec.b], writes=[(A2.b,)])
            S.op("dve", lambda e, j=j: e.tensor_tensor(out=G2.t[:, :, j], in0=mod4[:, 5, :, j], in1=gvec.t[:, 24:32],
                                                       op=ALU.mult), reads=[modT.b, gvec.b], writes=[(G2.b,)])

    for l in range(depth):
        W = WB[l]
        last = (l == depth - 1)
        need(l, "win")
        modT, A1, G1, A2, G2, mod4 = MODS[l % 2]
        if l == 0:
            phaseM(0)
            S.barrier()

        x_src = xT_in if l == 0 else xs
        if l > 0:
            S.op("dve", lambda e: e.memset(Vaug.t[:, :, :, 64:65], 1.0), writes=[Vaug.b])
            S.op("dve", lambda e: e.memset(ucT.t[:], 0.0), writes=[ucT.b])
        ringA = Ring(wpan, [pn for g in range(NG + 1) for pn in range(4)], lambda pn, wp, ex: S.dma(
            "sp", out=wp.t[:], in_=W["win"][:, pn * 512:(pn + 1) * 512].rearrange("(k p) n -> p k n", p=128),
            reads=[W["b_win"]], writes=[wp.b]))
        ringA.prime()
        xgA = [xg, xgA2]

        def loadA(g):
            if g == NG:
                return ctxT
            src = xgA[g % 2]
            S.dma("sp", out=src.t[:], in_=x_src[:, g * T:(g + 1) * T].rearrange("(k p) t -> p k t", p=128),
                  reads=[B_xs[g]], writes=[src.b])
            return src

        def prologueA(g):
            src = loadA(g)
            stats(src.t, CTX if g == NG else T, src.b, alt=True)
            return src

        srcT = prologueA(0)
        for g in range(NG + 1):
            isc = (g == NG)
            Tn = CTX if isc else T
            j = 1 if isc else 0
            src, srcb = srcT.t, srcT.b
            nxtT = loadA(g + 1) if g + 1 <= NG else None
            if g == 0:
                modulate(src, srcb, Tn, A1, lambda k, j=j, mod4=mod4: mod4[:, 0, k, j:j + 1], j)
            for pn in range(4):
                wp = ringA.get()
                if pn < 3:
                    for m in range(4):
                        ps, psb = rot()

                        def fn(e, m=m, ps=ps, wp=wp, Tn=Tn):
                            for k in range(8):
                                r = e.matmul(ps[:, :Tn], lhsT=wp.t[:, k, m * 128:(m + 1) * 128], rhs=hT.t[:, k, :Tn],
                                             start=(k == 0), stop=(k == 7))
                            return r
                        S.op("pe", fn, reads=[wp.b] + hT.cb, writes=[psb])
                        if pn == 0:
                            if isc:
                                S.op("dve", lambda e, m=m, ps=ps: e.tensor_copy(out=ucT.t[:, m, 8:8 + CTX], in_=ps[:, :CTX]),
                                     reads=[psb], writes=[(ucT.b,)])
                            else:
                                S.op("dve", lambda e, m=m, ps=ps: e.tensor_copy(out=uev.t[:, m, :], in_=ps[:, :]),
                                     reads=[psb], writes=[(uev.b,)])
                                S.op("dve", lambda e, m=m, ps=ps: e.tensor_copy(out=dT.t[:, m, :], in_=ps[:, :]),
                                     reads=[psb], writes=[dT.cb[m]])
                        elif pn == 1:
                            dst = qcT.t[:, m, :] if isc else qT.t[:, m, g * T:(g + 1) * T]
                            S.op("act", lambda e, ps=ps, dst=dst, Tn=Tn: e.activation(out=dst, in_=ps[:, :Tn], func=AF.Identity,
                                                                                      scale=0.125),
                                 reads=[psb], writes=[((qcT if isc else qT).b,)])
                        else:
                            dst = kcT.t[:, m, :] if isc else kT.t[:, m, g * T:(g + 1) * T]
                            S.op("dve", lambda e, ps=ps, dst=dst, Tn=Tn: e.tensor_copy(out=dst, in_=ps[:, :Tn]),
                                 reads=[psb], writes=[((kcT if isc else kT).b,)])
                    ringA.done()
                    if pn == 0 and not isc:
                        S.dma("sp", out=uT[:, :, 8 + g * T:8 + (g + 1) * T].rearrange("g p c -> p g c"), in_=uev.t[:],
                              reads=[uev.b], writes=[(B_uT,)])
                        S.dma("sp", out=uTb[:, :, g * T:(g + 1) * T].rearrange("g p c -> p g c"), in_=dT.t[:],
                              reads=dT.cb, writes=[(B_uTb,)])
                    if pn == 2 and g + 1 <= NG:
                        stats(nxtT.t, CTX if g + 1 == NG else T, nxtT.b, alt=True)
                        srcT = nxtT
                else:
                    ntt = Tn // 128
                    banks = [rot() for _ in range(ntt)]
                    jn = 1 if g + 1 == NG else 0
                    Tnn = CTX if g + 1 == NG else T
                    for k in range(8):
                        def fn(e, k=k, banks=banks, wp=wp, ntt=ntt):
                            for tt in range(ntt):
                                r = e.matmul(banks[tt][0][:, :], lhsT=hT.t[:, k, tt * 128:(tt + 1) * 128], rhs=wp.t[:, k, :],
                                             start=(k == 0), stop=(k == 7))
                            return r
                        S.op("pe", fn, reads=[wp.b, hT.cb[k]],
                             writes=[bk[1] for bk in banks] if k == 0 else [(bk[1],) for bk in banks])
                        if g + 1 <= NG:
                            modulate_chunk(k, srcT.t, srcT.b, Tnn, A1, lambda kk, jn=jn, mod4=mod4: mod4[:, 0, kk, jn:jn + 1], jn, hT)
                    for tt in range(ntt):
                        ps, psb = banks[tt]
                        Vt = Vcaug if isc else Vaug
                        vi = tt if isc else g * 4 + tt
                        S.op("act" if tt % 2 else "dve",
                             (lambda e, ps=ps, Vt=Vt, vi=vi: e.activation(
                                 out=Vt.t[:, vi, :, 0:64], in_=ps[:, :].rearrange("p (h d) -> p h d", d=64), func=AF.Identity))
                             if tt % 2 else
                             (lambda e, ps=ps, Vt=Vt, vi=vi: e.tensor_copy(
                                 out=Vt.t[:, vi, :, 0:64], in_=ps[:, :].rearrange("p (h d) -> p h d", d=64))),
                             reads=[psb], writes=[(Vt.b,)])
                    ringA.done()
            tick(2)
        S.barrier()

        for nm in ("pool", "bias", "wout"):
            need(l, nm)
        ngc = NG if last else NG + 1
        ring1 = Ring(wpan + [wpan4, wpan5], [jb for g in range(ngc) for jb in range(8)], lambda jb, wp, ex: S.dma(
            "sp", out=wp.t[:], in_=W["w1"][:, jb * 512:(jb + 1) * 512].rearrange("(k p) n -> p k n", p=128),
            reads=[W["b_w1"]], writes=[wp.b] + ex), alias=[bias_int.b, bias_bnd.b, pT[0].b, pT[1].b, pT[2].b, kT.b, Vaug.b])
        ring2 = Ring(w2pan, [m for g in range(ngc) for m in range(8)], lambda m, w2p, ex: S.dma(
            "act", out=w2p.t[:].rearrange("p j c -> p (j c)"), in_=W["w2"][m], reads=[W["b_w2"]], writes=[w2p.b] + ex),
            alias=[kT.b])
        S.dma("sp", out=poolT.t[:], in_=W["pool"].rearrange("(g p) n -> p g n", p=128), reads=[W["b_pool"]],
              writes=[poolT.b])
        for pg, w in enumerate((2, 4, 8, 16)):
            S.op("dve", lambda e, pg=pg, w=w: e.tensor_scalar(out=poolTs.t[:, pg, :], in0=poolT.t[:, pg, :], scalar1=1.0 / w,
                                                              scalar2=None, op0=ALU.mult),
                 reads=[poolT.b], writes=[poolTs.b] if pg == 0 else [(poolTs.b,)])
        S.op("dve", lambda e: e.tensor_scalar(out=poolT.t[:], in0=poolT.t[:], scalar1=-1.0, scalar2=None, op0=ALU.mult),
             reads=[poolT.b, poolTs.b], writes=[poolT.b])
        S.dma("sp", out=bias_int.t[:].rearrange("p a h q -> p (a h q)"), in_=W["bias"][:, 0:5 * 1024],
              reads=[W["b_bias"]], writes=[bias_int.b])
        S.op("act", lambda e: e.activation(out=bias_int.t[:].rearrange("p a h q -> p (a h q)"),
                                           in_=bias_int.t[:].rearrange("p a h q -> p (a h q)"), func=AF.Exp),
             reads=[bias_int.b], writes=[bias_int.b])

        def poolmix(usrc, usrcb, Tn, first, lastg, cast_u):
            Wd = Tn + 16
            for pg, w in enumerate((2, 4, 8, 16)):
                u = usrc[:, pg, :]
                levels = [(1, Wd, lambda i0, i1, u=u: (u[:, i0:i1], u[:, i0 - 1:i1 - 1]))]
                if w >= 4:
                    levels.append((2, Wd - 1, lambda i0, i1: (pta.t[:, i0 + 1:i1 + 1], pta.t[:, i0 - 1:i1 - 1])))
                if w >= 8:
                    levels.append((4, Wd - 3, lambda i0, i1: (ptb.t[:, i0 + 2:i1 + 2], ptb.t[:, i0 - 2:i1 - 2])))
                if w >= 16:
                    levels.append((8, Wd - 7, lambda i0, i1: (pta.t[:, i0 + 4:i1 + 4], pta.t[:, i0 - 4:i1 - 4])))
                bufs = [pta, ptb, pta, ptb]
                for li, (lo, hi, srcf) in enumerate(levels):
                    lastlev = (li == len(levels) - 1)
                    rd = [usrcb] if li == 0 else [bufs[li - 1].b]
                    if not lastlev:
                        dst = bufs[li]
                        a0, a1 = srcf(lo, hi)
                        S.op("pool", lambda e, dst=dst, lo=lo, hi=hi, a0=a0, a1=a1: e.tensor_tensor(
                            out=dst.t[:, lo:hi], in0=a0, in1=a1, op=ALU.add), reads=rd, writes=[dst.b])
                    else:
                        a0, a1 = srcf(8, 8 + Tn)
                        S.op("pool", lambda e, pg=pg, a0=a0, a1=a1: e.tensor_tensor(
                            out=dT.t[:, pg, :Tn], in0=a0, in1=a1, op=ALU.add), reads=rd, writes=[dT.cb[pg]])
                        for edge, on, c0 in ((0, first, 8), (1, lastg, 8 + Tn - 8)):
                            if on:
                                e0, e1 = srcf(c0, c0 + 8)
                                S.op("pool", lambda e, e0=e0, e1=e1: e.tensor_tensor(out=edg.t[:, :], in0=e0, in1=e1, op=ALU.add),
                                     reads=rd, writes=[edg.b])
                                S.op("pool", lambda e, edge=edge, pg=pg, c0=c0: e.tensor_tensor(
                                    out=dT.t[:, pg, c0 - 8:c0], in0=edg.t[:, :], in1=invc.t[:, edge, pg, :], op=ALU.mult),
                                    reads=[edg.b, invc.b], writes=[dT.cb[pg]])
            if cast_u:
                S.op("pool", lambda e: e.tensor_copy(out=dU.t[:, :, :Tn], in_=usrc[:, :, 8:8 + Tn]), reads=[usrcb], writes=[dU.b])

        def pool_proj(Tn):
            for pg in range(4):
                ps, psb = rot()

                def fn(e, pg=pg, ps=ps):
                    e.matmul(ps[:, :Tn], lhsT=poolTs.t[:, pg, :], rhs=dT.t[:, pg, :Tn], start=True, stop=False)
                    return e.matmul(ps[:, :Tn], lhsT=poolT.t[:, pg, :], rhs=dU.t[:, pg, :Tn], start=False, stop=True)
                S.op("pe", fn, reads=[poolT.b, poolTs.b, dT.cb[pg], dU.b], writes=[psb])
                S.op("dve", lambda e, pg=pg, ps=ps: e.tensor_scalar(out=catT.t[:, pg, :Tn], in0=ps[:, :Tn],
                                                                    scalar1=pscale.t[:, pg:pg + 1], scalar2=None, op0=ALU.mult),
                     reads=[psb, pscale.b], writes=[(catT.b,)])

        pend = []

        def flush_pending():
            while pend:
                pend.pop(0)()

        def attention(qsrc, q0, keys, nl, biasf, c0, rdb, fillers=()):
            nk = len(keys)

            def emit_S(h):
                c = h // 2
                pb = 64 * (h % 2)
                P = psS[h % 2]

                def fnS(e):
                    for i, (kt, kc0, Vt, vi) in enumerate(keys):
                        r = e.matmul(P.t[:, i * 128:(i + 1) * 128], lhsT=kt.t[pb:pb + 64, c, kc0:kc0 + 128],
                                     rhs=qsrc.t[pb:pb + 64, c, q0:q0 + 128], start=True, stop=True)
                    return r
                S.op("pe", fnS, reads=[qT.b, kT.b, kcT.b, qcT.b], writes=[BK[2 * (h % 2)], BK[2 * (h % 2) + 1]])

            def emit_soft(h):
                sbi = h % 2
                P = psS[sbi]
                Pb = [BK[2 * sbi], BK[2 * sbi + 1]]
                pt = pT[h % 3]
                S.op("act", lambda e: e.activation(out=pt.t[:, 0:nk * 128], in_=P.t[:, 0:nk * 128], func=AF.Exp),
                     reads=Pb, writes=[pt.b])
                if nl:
                    S.op("dve", lambda e: e.tensor_tensor(
                        out=pt.t[:, 0:nl * 128].rearrange("p (a q) -> p a q", q=128),
                        in0=pt.t[:, 0:nl * 128].rearrange("p (a q) -> p a q", q=128), in1=biasf(h), op=ALU.mult),
                        reads=[pt.b, rdb], writes=[pt.b])

            def emit_PV(h):
                pt = pT[h % 3]
                hh = h % 4

                def fnPV(e):
                    for i, (kt, kc0, Vt, vi) in enumerate(keys):
                        r = e.matmul(psO[h // 4].t[:, hh * 65:hh * 65 + 65], lhsT=pt.t[:, i * 128:(i + 1) * 128],
                                     rhs=Vt.t[:, vi, h, :], start=(i == 0), stop=(i == nk - 1))
                    return r
                S.op("pe", fnPV, reads=[pt.b, Vaug.b, Vcaug.b], writes=[BK[4 + h // 4]] if h % 4 == 0 else [(BK[4 + h // 4],)])

            fillers = list(fillers)
            emit_S(0)
            emit_soft(0)
            emit_S(1)
            emit_soft(1)
            flush_pending()
            for h in range(2, 8):
                emit_S(h)
                emit_PV(h - 2)
                emit_soft(h)
                if fillers:
                    fillers.pop(0)()
            emit_PV(6)
            emit_PV(7)
            while fillers:
                fillers.pop(0)()
            for hf in range(2):
                S.op("dve", lambda e, hf=hf: e.reciprocal(
                    out=rinv.t[:, hf * 4:(hf + 1) * 4],
                    in_=psO[hf].t[:, 0:260].rearrange("p (h d) -> p h d", d=65)[:, :, 64]),
                    reads=[BK[4 + hf]], writes=[(rinv.b,)])
            for hf in range(2):
                S.op("dve", lambda e, hf=hf: e.tensor_tensor(
                    out=On.t[:, hf * 256:(hf + 1) * 256].rearrange("p (h d) -> p h d", d=64),
                    in0=psO[hf].t[:, 0:260].rearrange("p (h d) -> p h d", d=65)[:, :, 0:64],
                    in1=rinv.t[:, hf * 4:(hf + 1) * 4].unsqueeze(2).broadcast_to([128, 4, 64]),
                    op=ALU.mult), reads=[BK[4 + hf], rinv.b], writes=[(On.b,)])

            def tail():
                def fnT(e):
                    for c in range(4):
                        r = e.transpose(out=psT.t[:, c * 128:(c + 1) * 128], in_=On.t[:, c * 128:(c + 1) * 128],
                                        identity=ident.t[:])
                    return r
                S.op("pe", fnT, reads=[On.b, ident.b], writes=[psT.b])
                S.op("dve", lambda e: e.tensor_copy(out=catT.t[:, 4:8, c0:c0 + 128],
                                                    in_=psT.t[:, 0:512].rearrange("p (c q) -> p c q", q=128)),
                     reads=[psT.b], writes=[(catT.b,)])
            pend.append(tail)

        ringO = Ring(wop, [m for g in range(NG + (0 if last else 1)) for m in range(8)], lambda m, wp, ex: S.dma(
            "act", out=wp.t[:].rearrange("p k c -> p (k c)"), in_=W["wout"][m], reads=[W["b_wout"]], writes=[wp.b]))
        ringO.prime()

        def outproj(Tn, j, g, G1=G1):
            accs = []
            for m in range(8):
                wp = ringO.get()
                ps, psb = rot()

                def fn(e, ps=ps, wp=wp):
                    for kc in range(8):
                        r = e.matmul(ps[:, :Tn], lhsT=wp.t[:, kc, :], rhs=catT.t[:, kc, :Tn],
                                     start=(kc == 0), stop=(kc == 7))
                    return r
                S.op("pe", fn, reads=[wp.b, catT.b], writes=[psb])
                ringO.done()
                S.op("dve", lambda e, m=m, ps=ps: e.tensor_copy(out=mixg.t[:, m, :Tn], in_=ps[:, :Tn]),
                     reads=[psb], writes=[mixg.cb[m]])
                if accs:
                    accs.pop(0)()
                accs.append(stats_sq(m, mixg.t[:, m, :Tn], mixg.cb[m], Tn, eng="dve"))
            accs.pop(0)()
            if g is None:
                stats_finish(Tn)
                residual_update(mixg, G1, j, ctxT.t, ctxT.b, Tn, ctxT.t, ctxT.b, tmps=xr[:2])
                return None

            def load_xr(m):
                S.dma("sp", out=xr[m % 3].t[:], in_=x_src[m * 128:(m + 1) * 128, g * T:(g + 1) * T], reads=[B_xs[g]],
                      writes=[xr[m % 3].b])

            def chunk(m):
                if m == 0:
                    for mm in range(3):
                        load_xr(mm)
                xc = xr[m % 3]
                S.op("dve", lambda e: e.scalar_tensor_tensor(
                    out=mixg.t[:, m, :Tn], in0=mixg.t[:, m, :Tn], scalar=G1.t[:, m, j:j + 1], in1=rstd.t[:, :Tn],
                    op0=ALU.mult, op1=ALU.mult), reads=[mixg.cb[m], G1.b, rstd.b], writes=[mixg.cb[m]])
                S.op("pool", lambda e: e.tensor_tensor(out=xc.t[:, :Tn], in0=mixg.t[:, m, :Tn], in1=xc.t[:, :Tn],
                                                       op=ALU.add), reads=[mixg.cb[m], xc.b], writes=[xc.b])
                S.dma("sp", out=xs[m * 128:(m + 1) * 128, g * T:(g + 1) * T], in_=xc.t[:], reads=[xc.b],
                      writes=[(B_xs[g],)])
                if m + 3 < 8:
                    load_xr(m + 3)
            return [lambda: stats_finish(Tn)] + [(lambda m=m: chunk(m)) for m in range(8)]

        ckeys = [(kcT, i * 128, Vcaug, i) for i in range(CTX // 128)]
        kv_b = [qT.b, kT.b, Vaug.b, kcT.b, Vcaug.b, qcT.b]
        def load_ug(g):
            S.dma("sp", out=ug0.t[:], in_=uT[:, :, g * T:g * T + T + 16].rearrange("g p c -> p g c"),
                  reads=[B_uT], writes=[ug0.b])

        def load_bnd(jq):
            bj = {0: 0, 1: 1, 30: 2, 31: 3}[jq]
            S.dma("sp", out=bias_bnd.t[:].rearrange("p a h q -> p (a h q)"),
                  in_=W["bias"][:, (5 + 4 * bj) * 1024:(9 + 4 * bj) * 1024], reads=[W["b_bias"]], writes=[bias_bnd.b])
            S.op("act", lambda e: e.activation(out=bias_bnd.t[:].rearrange("p a h q -> p (a h q)"),
                                               in_=bias_bnd.t[:].rearrange("p a h q -> p (a h q)"), func=AF.Exp),
                 reads=[bias_bnd.b], writes=[bias_bnd.b])

        def load_dU(g):
            S.dma("sp", out=dU.t[:], in_=uTb[:, :, g * T:(g + 1) * T].rearrange("g p c -> p g c"), reads=[B_uTb],
                  writes=[dU.b])

        resid_pending = []
        load_bnd(0)
        load_ug(0)
        load_dU(0)
        poolmix(ug0.t, ug0.b, T, True, False, False)
        for g in range(NG):
            order = {0: [0, 2, 3, 1], NG - 1: [2, 0, 1, 3]}.get(g, [0, 1, 2, 3])
            for jj in order:
                jq = g * 4 + jj
                if 2 <= jq <= 29:
                    keys = [(kT, (jq + dt_) * 128, Vaug, jq + dt_) for dt_ in range(-2, 3)]
                    biasf = lambda h: bias_int.t[:, :, h, :]
                    rdb = bias_int.b
                else:
                    t0 = 0 if jq < 2 else 28
                    keys = [(kT, (t0 + i) * 128, Vaug, t0 + i) for i in range(4)]
                    biasf = lambda h: bias_bnd.t[:, :, h, :]
                    rdb = bias_bnd.b
                fl = resid_pending.pop(0) if (jj == order[1] and resid_pending) else ()
                attention(qT, jq * 128, keys + ckeys, len(keys), biasf, jj * 128, rdb, fillers=fl)
                tick(1)
                nxt_b = {0: 1, 1: 30, 30: 31}.get(jq)
                if nxt_b is not None:
                    load_bnd(nxt_b)
                if jq == 0:
                    load_ug(1)
            flush_pending()
            if g == NG - 1:
                need(l, "w1")
                need(l, "w2")
                ring2.prime()
                if last:
                    ring1.prime()
            pool_proj(T)
            resid_pending.append(outproj(T, 0, g))
            if g + 1 < NG:
                load_dU(g + 1)
                poolmix(ug0.t, ug0.b, T, False, g + 1 == NG - 1, False)
            if g + 2 < NG:
                load_ug(g + 2)
        lastres = resid_pending.pop(0)
        if last:
            for f_ in lastres:
                f_()
        if not last:
            poolmix(ucT.t, ucT.b, CTX, True, True, True)
            for jj in range(CTX // 128):
                fl = lastres[:6] if jj == 0 else lastres[6:]
                attention(qcT, jj * 128, ckeys, 0, None, jj * 128, None, fillers=fl)
            flush_pending()
            ring1.prime()
            pool_proj(CTX)
            outproj(CTX, 1, None)
        S.barrier()

        ring2.prime_rest()
        ring1.prime_rest()
        xgC = [xg, xgC2]
        hTC = [hT, hTC2]

        def loadC(g):
            if g == NG:
                return ctxT
            src = xgC[g % 2]
            S.dma("sp", out=src.t[:], in_=xs[:, g * T:(g + 1) * T].rearrange("(k p) t -> p k t", p=128),
                  reads=[B_xs[g]], writes=[src.b])
            return src

        def squaresC(g, src):
            Tn = CTX if g == NG else T
            for k in range(8):
                S.op("pool" if k % 2 == 0 else "dve", lambda e, k=k, src=src, Tn=Tn: e.tensor_tensor(
                    out=sqC.t[:, k, :Tn], in0=src.t[:, k, :Tn], in1=src.t[:, k, :Tn], op=ALU.mult),
                    reads=[src.b], writes=[sqC.cb[k]])

        def restC(g, src):
            Tn = CTX if g == NG else T
            j = 1 if g == NG else 0

            def fn(e):
                for k in range(8):
                    r = e.matmul(psX.t[:, :Tn], lhsT=ones.t[:], rhs=sqC.t[:, k, :Tn], start=(k == 0), stop=(k == 7))
                return r
            S.op("pe", fn, reads=sqC.cb + [ones.b], writes=[psX.b])
            stats_finish(Tn)
            modulate(src.t, src.b, Tn, A2, lambda k, j=j, mod4=mod4: mod4[:, 3, k, j:j + 1], j, hT=hTC[g % 2])
            return src

        def prologueC(g):
            src = loadC(g)
            squaresC(g, src)
            return restC(g, src)

        srcT = prologueC(0)
        for g in range(ngc):
            isc = (g == NG)
            Tn = CTX if isc else T
            hTg = hTC[g % 2]
            cur = srcT
            nxtT = None
            for jb in range(8):
                if jb == 2 and g + 1 < ngc:
                    nxtT = loadC(g + 1)
                if jb == 7 and g + 1 < ngc:
                    squaresC(g + 1, nxtT)
                wp = ring1.get()
                for jj in range(4):
                    ps, psb = rot()

                    def fn(e, jj=jj, ps=ps, wp=wp, Tn=Tn, hTg=hTg):
                        for k in range(8):
                            r = e.matmul(ps[:, :Tn], lhsT=wp.t[:, k, jj * 128:(jj + 1) * 128], rhs=hTg.t[:, k, :Tn],
                                         start=(k == 0), stop=(k == 7))
                        return r
                    S.op("pe", fn, reads=[wp.b] + hTg.cb, writes=[psb])
                    tf = rtmp[jj % 4]
                    S.op("act", lambda e, ps=ps, tf=tf, Tn=Tn: e.activation(out=tf.t[:, :Tn], in_=ps[:, :Tn], func=AF.Relu),
                         reads=[psb], writes=[tf.b])
                    S.op("pool" if jj % 2 else "dve",
                         lambda e, tf=tf, jb=jb, jj=jj, Tn=Tn: e.tensor_tensor(out=actT.t[:, jb * 4 + jj, :Tn], in0=tf.t[:, :Tn],
                                                                             in1=tf.t[:, :Tn], op=ALU.mult),
                         reads=[tf.b], writes=[(actT.b,)])
                ring1.done()
            if g + 1 < ngc:
                srcT = restC(g + 1, nxtT)
            elif l + 1 < depth:
                phaseM(l + 1)
            accs = []
            for m in range(8):
                w2p = ring2.get()
                ps, psb = rot()

                def fn(e, ps=ps, w2p=w2p, Tn=Tn):
                    for jx in range(32):
                        r = e.matmul(ps[:, :Tn], lhsT=w2p.t[:, jx, :], rhs=actT.t[:, jx, :Tn], start=(jx == 0), stop=(jx == 31))
                    return r
                S.op("pe", fn, reads=[w2p.b, actT.b], writes=[psb])
                ring2.done()
                if accs:
                    accs.pop(0)()
                S.op("act", lambda e, m=m, ps=ps, Tn=Tn: e.activation(out=mixg.t[:, m, :Tn], in_=ps[:, :Tn], func=AF.Identity),
                     reads=[psb], writes=[mixg.cb[m]])
                accs.append(stats_sq(m, mixg.t[:, m, :Tn], mixg.cb[m], Tn))
            accs.pop(0)()
            stats_finish(Tn)
            if isc:
                residual_update(mixg, G2, 1, ctxT.t, ctxT.b, Tn, ctxT.t, ctxT.b)
            else:
                residual_update(mixg, G2, 0, cur.t, cur.b, Tn, cur.t, cur.b)
                dst = yT if last else xs
                S.dma("sp", out=dst[:, g * T:(g + 1) * T].rearrange("(k p) t -> p k t", p=128), in_=cur.t[:],
                      reads=[cur.b], writes=[B_xs[g]])
            tick(2)
        S.barrier()

    with nc.Block() as block:
        @block.sync
        def _(e):
            for th in S.ops["sp"]:
                th(e)

        @block.tensor
        def _(e):
            for th in S.ops["pe"]:
                th(e)

        @block.scalar
        def _(e):
            for th in S.ops["act"]:
                th(e)

        @block.vector
        def _(e):
            for th in S.ops["dve"]:
                th(e)

        @block.gpsimd
        def _(e):
            for th in S.ops["pool"]:
                th(e)
    return nc


def _bias_tiles(rpb):
    depth = rpb.shape[0]
    kl = np.arange(128) // 64
    kc = np.arange(128) % 64
    rl = np.arange(128) // 64
    c = np.arange(128) % 64
    cs = np.clip(c - 8, 0, GRID_W - 16)
    out = np.empty((depth, 128, 21, 8, 128), np.float32)

    def tile(j, t):
        kr = 2 * t + kl
        r = 2 * j + rl
        rs = np.clip(r - 4, 0, 64 - 8)
        okr = (kr[:, None] >= rs[None, :]) & (kr[:, None] < rs[None, :] + 8)
        okc = (kc[:, None] >= cs[None, :]) & (kc[:, None] < cs[None, :] + 16)
        dr = np.clip(kr[:, None] - r[None, :] + 7, 0, 14)
        dc = np.clip(kc[:, None] - c[None, :] + 15, 0, 30)
        vals = rpb[:, :, dr, dc]
        vals = np.where((okr & okc)[None, None], vals, np.float32(NEG))
        return np.transpose(vals, (0, 2, 1, 3))

    for i, dt_ in enumerate(range(-2, 3)):
        out[:, :, i] = tile(10, 10 + dt_)
    for bj, j in enumerate((0, 1, 30, 31)):
        t0 = 0 if j < 2 else 28
        for i in range(4):
            out[:, :, 5 + 4 * bj + i] = tile(j, t0 + i)
    return out.reshape(depth, 128, 21 * 8 * 128)


def _invc():
    out = np.ones((2, 4, 8), np.float32)
    Lh = 1024
    for pg, w in enumerate((2, 4, 8, 16)):
        lo = w // 2
        hi = w - 1 - lo
        for i in range(8):
            t = i
            cnt = min(t + hi + 1, Lh) - max(t - lo, 0)
            out[0, pg, i] = np.float32(w) / np.float32(cnt)
            t = Lh - 8 + i
            cnt = min(t + hi + 1, Lh) - max(t - lo, 0)
            out[1, pg, i] = np.float32(w) / np.float32(cnt)
    return np.broadcast_to(out.reshape(1, 64), (128, 64)).copy()


def prep_shared(depth, w_mod, b_mod, g_pre_mix, g_post_mix, g_pre_mlp, g_post_mlp, w_in, pool_w, pool_scale, rpb,
                w_out, w_mlp_in, w_mlp_out):
    f = np.float32

    def pk(v, n):
        return np.ascontiguousarray(np.transpose(v.reshape(depth, n, 128), (0, 2, 1)))
    bm = pk(np.asarray(b_mod[:depth], f), 48)
    gv = np.concatenate([pk(np.asarray(a[:depth], f), 8) for a in (g_pre_mix, g_post_mix, g_pre_mlp, g_post_mlp)], axis=2)
    return {
        "ident": np.eye(128, dtype=f),
        "invc": _invc(),
        "w_mod": np.ascontiguousarray(w_mod[:depth], f),
        "b_modT": np.ascontiguousarray(np.repeat(bm, 2, axis=2)),
        "gvec": np.ascontiguousarray(gv),
        "pscale": pk(np.asarray(pool_scale[:depth], f), 4),
        "w_in": np.ascontiguousarray(w_in[:depth], f),
        "pool_w": np.ascontiguousarray(np.asarray(pool_w[:depth], f).reshape(depth, 512, 128)),
        "biasT": _bias_tiles(np.asarray(rpb[:depth], f)),
        "w_out": np.ascontiguousarray(w_out[:depth], f),
        "w1": np.ascontiguousarray(w_mlp_in[:depth], f),
        "w2": np.ascontiguousarray(w_mlp_out[:depth], f),
    }


def prep_core(x_b, c_b, ctx_b, c_ctx):
    f = np.float32
    cc = np.stack([np.asarray(c_b, f), np.asarray(c_ctx, f)], axis=1)
    cc = np.transpose(cc.reshape(8, 128, 2), (1, 0, 2)).reshape(128, 16)
    return {
        "xT": np.ascontiguousarray(np.asarray(x_b, f).T),
        "ctxT": np.ascontiguousarray(np.asarray(ctx_b, f).T),
        "cc": np.ascontiguousarray(cc),
    }


_NC_CACHE = {}


def kernel(x, c, ctx, c_ctx, w_mod, b_mod, g_pre_mix, g_post_mix, g_pre_mlp, g_post_mlp,
           w_in, pool_w, pool_scale, rpb, w_out, w_mlp_in, w_mlp_out):
    x = np.asarray(x)
    nb = x.shape[0]
    shared = prep_shared(DEPTH, np.asarray(w_mod), np.asarray(b_mod), np.asarray(g_pre_mix), np.asarray(g_post_mix),
                         np.asarray(g_pre_mlp), np.asarray(g_post_mlp), np.asarray(w_in), np.asarray(pool_w),
                         np.asarray(pool_scale), np.asarray(rpb), np.asarray(w_out), np.asarray(w_mlp_in),
                         np.asarray(w_mlp_out))
    in_maps = []
    for b in range(nb):
        m = dict(shared)
        m.update(prep_core(x[b], np.asarray(c)[b], np.asarray(ctx)[b], np.asarray(c_ctx)))
        in_maps.append(m)
    nc = build(DEPTH)
    res = run_bass_kernel_spmd(nc, in_maps, core_ids=list(range(nb)))
    out = np.stack([np.ascontiguousarray(r["yT"].T) for r in res.results], axis=0)
    return out.astype(np.float32)
```
